# Optimizing a Trainium2 kernel written in Bass

```python
import math
import jax
import jax.numpy as jnp
from jax import lax
import numpy as np

D_MODEL = 1024
BATCH = 8
SEQ = 2048
DEPTH = 2

CTX_LEN = 256
GRID_W = 64
EPS = 1e-6
N_MOD = 6

D_RNN = 1024
RNN_BLOCKS = 8
RNN_BLOCK = D_RNN // RNN_BLOCKS
CONV_W = 4
LRU_C = 8.0

FOURIER_GROUPS = 8
FOURIER_GD = 128
D_FOURIER = FOURIER_GROUPS * FOURIER_GD

N_HEADS = 8
HEAD_DIM = 64
V_DIM = 2 * HEAD_DIM
D_ATTN = N_HEADS * V_DIM
ROPE_BASE = 10000.0
Q_BLOCK = 128

N_BRANCH = 3
D_MIX = D_RNN + D_FOURIER + D_ATTN
D_IN = 2 * D_RNN + D_FOURIER + 3 * D_ATTN
SPLITS = (D_RNN, 2 * D_RNN, 2 * D_RNN + D_FOURIER, 2 * D_RNN + D_FOURIER + D_ATTN, 2 * D_RNN + D_FOURIER + 2 * D_ATTN)
D_FF = ((8 * D_MODEL + 3 * 256 - 1) // (3 * 256)) * 256

kernel_name = 'hybrid_rglru_fourier_diffattn_dit'


def rmsnorm(x, g=None):
    xf = x.astype(jnp.float32)
    y = xf * lax.rsqrt(jnp.mean(xf * xf, axis=-1, keepdims=True) + EPS)
    if g is not None:
        y = y * g.astype(jnp.float32)
    return y.astype(x.dtype)


def adaln_params(cond, w, b):
    m = jax.nn.silu(cond) @ w + b
    return jnp.split(m[..., None, :], N_MOD, axis=-1)


def modulate(x, shift, scale):
    return rmsnorm(x) * (1 + scale) + shift


def dwconv_centred(x, w, b):
    n = x.shape[1]
    left = CONV_W // 2
    xp = jnp.pad(x, ((0, 0), (left, CONV_W - 1 - left), (0, 0)))
    out = xp[:, 0:n] * w[0]
    for k in range(1, CONV_W):
        out = out + xp[:, k:k + n] * w[k]
    return out + b


def block_diag_linear(x, w, b):
    xb = x.reshape(x.shape[:-1] + (RNN_BLOCKS, RNN_BLOCK))
    y = jnp.einsum('blgi,gij->blgj', xb, w.astype(jnp.float32))
    return y.reshape(x.shape) + b.astype(jnp.float32)


def rglru_coeffs(u, wr, br, wi, bi, lam):
    uf = u.astype(jnp.float32)
    r = jax.nn.sigmoid(block_diag_linear(uf, wr, br))
    i = jax.nn.sigmoid(block_diag_linear(uf, wi, bi))
    log_a = LRU_C * r * jax.nn.log_sigmoid(lam.astype(jnp.float32))
    a = jnp.exp(log_a)
    mult = jnp.sqrt(-jnp.expm1(2.0 * log_a))
    return a, mult * i * uf


def _lru_combine(e1, e2):
    a1, b1 = e1
    a2, b2 = e2
    return a1 * a2, a2 * b1 + b2


def linear_scan(a, b, h0, reverse):
    if h0 is not None:
        edge = -1 if reverse else 0
        b = b.at[:, edge].add(a[:, edge] * h0)
    _, h = lax.associative_scan(_lru_combine, (a, b), reverse=reverse, axis=1)
    return h


def rglru_mixer(xr_c, gr_c, xr_l, gr_l, conv_w, conv_b, wr, br, wi, bi, lam, need_ctx_out):
    uc = dwconv_centred(xr_c, conv_w, conv_b)
    ul = dwconv_centred(xr_l, conv_w, conv_b)
    lat_dirs, ctx_dirs = [], []
    for d, reverse in ((0, False), (1, True)):
        ac, bc = rglru_coeffs(uc, wr[d], br[d], wi[d], bi[d], lam[d])
        hc = linear_scan(ac, bc, None, reverse)
        state = hc[:, 0] if reverse else hc[:, -1]
        al, bl = rglru_coeffs(ul, wr[d], br[d], wi[d], bi[d], lam[d])
        lat_dirs.append(linear_scan(al, bl, state, reverse))
        ctx_dirs.append(hc)
    yl = ((lat_dirs[0] + lat_dirs[1]) * jax.nn.gelu(gr_l.astype(jnp.float32))).astype(xr_l.dtype)
    if not need_ctx_out:
        return yl, None
    yc = ((ctx_dirs[0] + ctx_dirs[1]) * jax.nn.gelu(gr_c.astype(jnp.float32))).astype(xr_c.dtype)
    return yl, yc


def fourier_mix(xf):
    b, n, _ = xf.shape
    xg = xf.astype(jnp.float32).reshape(b, n, FOURIER_GROUPS, FOURIER_GD)
    y = jnp.fft.fft2(xg, axes=(1, 3), norm='ortho').real
    return y.reshape(b, n, D_FOURIER).astype(xf.dtype)


def axial_rope(n_lat):
    rows_count = n_lat // GRID_W
    rows = jnp.repeat(jnp.arange(rows_count), GRID_W).astype(jnp.float32)
    cols = jnp.tile(jnp.arange(GRID_W), rows_count).astype(jnp.float32)
    n_freq = HEAD_DIM // 4
    inv = ROPE_BASE ** (-jnp.arange(n_freq, dtype=jnp.float32) / n_freq)
    ang = jnp.concatenate([rows[:, None] * inv, cols[:, None] * inv], axis=-1)
    return jnp.cos(ang), jnp.sin(ang)


def apply_rope(t, cos, sin):
    c = cos[None, :, None, None, :]
    s = sin[None, :, None, None, :]
    t1, t2 = jnp.split(t.astype(jnp.float32), 2, axis=-1)
    return jnp.concatenate([t1 * c - t2 * s, t1 * s + t2 * c], axis=-1).astype(t.dtype)


def split_qk_heads(t):
    return t.reshape(t.shape[:2] + (N_HEADS, 2, HEAD_DIM))


def split_v_heads(t):
    return t.reshape(t.shape[:2] + (N_HEADS, V_DIM))


def diff_attend(q, k, v, lam, lam_init):
    s = jnp.einsum('bqhcd,bkhcd->bhcqk', q.astype(jnp.float32), k.astype(jnp.float32)) * HEAD_DIM ** -0.5
    p = jax.nn.softmax(s, axis=-1)
    w = p[:, :, 0] - lam * p[:, :, 1]
    o = jnp.einsum('bhqk,bkhv->bqhv', w, v.astype(jnp.float32))
    o = o * lax.rsqrt(jnp.mean(o * o, axis=-1, keepdims=True) + EPS)
    return o * (1.0 - lam_init)


def diff_attention(q_c, k_c, v_c, q_l, k_l, v_l, lam_vec, lam_init, need_ctx_out):
    b, n_lat = q_l.shape[0], q_l.shape[1]
    cos, sin = axial_rope(n_lat)
    ql = apply_rope(split_qk_heads(q_l), cos, sin)
    kl = apply_rope(split_qk_heads(k_l), cos, sin)
    kc = split_qk_heads(k_c)
    vc = split_v_heads(v_c)
    k_all = jnp.concatenate([kl, kc], axis=1)
    v_all = jnp.concatenate([split_v_heads(v_l), vc], axis=1)
    lv = lam_vec.astype(jnp.float32)
    lam = jnp.exp(jnp.sum(lv[0] * lv[1])) - jnp.exp(jnp.sum(lv[2] * lv[3])) + lam_init
    n_blk = n_lat // Q_BLOCK
    qb = ql.reshape(b, n_blk, Q_BLOCK, N_HEADS, 2, HEAD_DIM).swapaxes(0, 1)
    ob = lax.map(lambda qi: diff_attend(qi, k_all, v_all, lam, lam_init), qb)
    yl = ob.swapaxes(0, 1).reshape(b, n_lat, D_ATTN).astype(q_l.dtype)
    if not need_ctx_out:
        return yl, None
    yc = diff_attend(split_qk_heads(q_c), kc, vc, lam, lam_init)
    return yl, yc.reshape(b, q_c.shape[1], D_ATTN).astype(q_c.dtype)


def merge_branches(h, ya, yb, yc, w_branch, w_gate, b_gate, w_out):
    pa = ya @ w_branch[:D_RNN]
    pb = yb @ w_branch[D_RNN:D_RNN + D_FOURIER]
    pc = yc @ w_branch[D_RNN + D_FOURIER:]
    g = jax.nn.sigmoid((h @ w_gate + b_gate).astype(jnp.float32)).astype(h.dtype)
    ga, gb, gc = jnp.split(g, N_BRANCH, axis=-1)
    return (ga * pa + gb * pb + gc * pc) @ w_out


def swiglu(h, w1, w3, w2):
    return (jax.nn.silu(h @ w1) * (h @ w3)) @ w2


def setup_inputs(seed: int = 0) -> dict:
    key = jax.random.key(seed)
    ks = jax.random.split(key, 24)
    f32 = jnp.float32

    def nrm(k, shape, fan_in, gain=1.0):
        return gain * fan_in ** -0.5 * jax.random.normal(k, shape, f32)

    def small(k, shape):
        return 0.02 * jax.random.normal(k, shape, f32)

    u = jax.random.uniform(ks[13], (DEPTH, 2, D_RNN), f32, 0.9, 0.999)
    a_base = u ** (1.0 / LRU_C)
    return {
        'x': jax.random.normal(ks[0], (BATCH, SEQ, D_MODEL), f32),
        'c': jax.random.normal(ks[1], (BATCH, D_MODEL), f32),
        'ctx': jax.random.normal(ks[2], (BATCH, CTX_LEN, D_MODEL), f32),
        'c_ctx': jax.random.normal(ks[3], (D_MODEL,), f32),
        'ada_w': nrm(ks[4], (DEPTH, D_MODEL, N_MOD * D_MODEL), D_MODEL, 0.5),
        'ada_b': small(ks[5], (DEPTH, N_MOD * D_MODEL)),
        'w_in': nrm(ks[6], (DEPTH, D_MODEL, D_IN), D_MODEL),
        'rnn_conv_w': nrm(ks[7], (DEPTH, CONV_W, D_RNN), CONV_W),
        'rnn_conv_b': small(ks[8], (DEPTH, D_RNN)),
        'rnn_wr': nrm(ks[9], (DEPTH, 2, RNN_BLOCKS, RNN_BLOCK, RNN_BLOCK), RNN_BLOCK),
        'rnn_br': small(ks[10], (DEPTH, 2, D_RNN)),
        'rnn_wi': nrm(ks[11], (DEPTH, 2, RNN_BLOCKS, RNN_BLOCK, RNN_BLOCK), RNN_BLOCK),
        'rnn_bi': small(ks[12], (DEPTH, 2, D_RNN)),
        'rnn_lambda': jnp.log(a_base) - jnp.log1p(-a_base),
        'attn_lambda': 0.1 * jax.random.normal(ks[14], (DEPTH, 4, HEAD_DIM), f32),
        'w_branch': nrm(ks[15], (DEPTH, D_MIX, D_MODEL), D_RNN),
        'w_gate': nrm(ks[16], (DEPTH, D_MODEL, N_BRANCH * D_MODEL), D_MODEL),
        'b_gate': small(ks[17], (DEPTH, N_BRANCH * D_MODEL)),
        'w_out': nrm(ks[18], (DEPTH, D_MODEL, D_MODEL), D_MODEL),
        'ffn_w1': nrm(ks[19], (DEPTH, D_MODEL, D_FF), D_MODEL),
        'ffn_w3': nrm(ks[20], (DEPTH, D_MODEL, D_FF), D_MODEL),
        'ffn_w2': nrm(ks[21], (DEPTH, D_FF, D_MODEL), D_FF),
        'final_g': 1.0 + small(ks[22], (D_MODEL,)),
    }


def reference(x, c, ctx, c_ctx, ada_w, ada_b, w_in, rnn_conv_w, rnn_conv_b, rnn_wr, rnn_br, rnn_wi, rnn_bi, rnn_lambda, attn_lambda, w_branch, w_gate, b_gate, w_out, ffn_w1, ffn_w3, ffn_w2, final_g):
    xl, xc = x, ctx
    for l in range(DEPTH):
        last = l == DEPTH - 1
        lam_init = 0.8 - 0.6 * math.exp(-0.3 * l)
        sh1_l, sc1_l, g1_l, sh2_l, sc2_l, g2_l = adaln_params(c, ada_w[l], ada_b[l])
        sh1_c, sc1_c, g1_c, sh2_c, sc2_c, g2_c = adaln_params(c_ctx, ada_w[l], ada_b[l])

        hl = modulate(xl, sh1_l, sc1_l)
        hc = modulate(xc, sh1_c, sc1_c)
        xr_l, gr_l, xf_l, q_l, k_l, v_l = jnp.split(hl @ w_in[l], SPLITS, axis=-1)
        if last:
            xr_c = hc @ w_in[l][:, :SPLITS[0]]
            k_c, v_c = jnp.split(hc @ w_in[l][:, SPLITS[3]:], 2, axis=-1)
            gr_c = xf_c = q_c = None
        else:
            xr_c, gr_c, xf_c, q_c, k_c, v_c = jnp.split(hc @ w_in[l], SPLITS, axis=-1)

        ya_l, ya_c = rglru_mixer(xr_c, gr_c, xr_l, gr_l, rnn_conv_w[l], rnn_conv_b[l], rnn_wr[l], rnn_br[l], rnn_wi[l], rnn_bi[l], rnn_lambda[l], not last)
        yc_l, yc_c = diff_attention(q_c, k_c, v_c, q_l, k_l, v_l, attn_lambda[l], lam_init, not last)
        yb_l = fourier_mix(xf_l)
        xl = xl + g1_l * merge_branches(hl, ya_l, yb_l, yc_l, w_branch[l], w_gate[l], b_gate[l], w_out[l])

        xl = xl + g2_l * swiglu(modulate(xl, sh2_l, sc2_l), ffn_w1[l], ffn_w3[l], ffn_w2[l])

        if not last:
            yb_c = fourier_mix(xf_c)
            xc = xc + g1_c * merge_branches(hc, ya_c, yb_c, yc_c, w_branch[l], w_gate[l], b_gate[l], w_out[l])
            xc = xc + g2_c * swiglu(modulate(xc, sh2_c, sc2_c), ffn_w1[l], ffn_w3[l], ffn_w2[l])

    return rmsnorm(xl, final_g)
```

```python
import math
import dataclasses
import numpy as np
import ml_dtypes
import concourse.bass as bass
import concourse.mybir as mybir
from concourse.bass_utils import run_bass_kernel_spmd

F32 = mybir.dt.float32
F32R = mybir.dt.float32r
BF16 = mybir.dt.bfloat16
AF = mybir.ActivationFunctionType
ALU = mybir.AluOpType

D = 1024
SEQ = 2048
CTX = 256
T = SEQ + CTX
DEPTH = 2
DFF = 2816
NFF = DFF // 128
EPS = 1e-6
CB = [(0, 256), (256, 512), (768, 512), (1280, 512), (1792, 512)]
GW = 2307
GB = [(0, 512), (512, 512), (1024, 512), (1536, 512), (2048, 259)]
ARENA0 = 16896
DTSIZE = {F32: 4, F32R: 4, BF16: 2}
ARENA1 = 229376


class Buf:
    __slots__ = ("lw", "rd")

    def __init__(self):
        self.lw = None
        self.rd = []


class Tile:
    def __init__(self, h, psum=False):
        self.h = h
        self.buf = Buf()
        self.psum = psum

    def __getitem__(self, k):
        return self.h[k]


class Op:
    __slots__ = ("id", "stream", "is_dma", "fn", "waits", "signal", "sigval", "dsem", "dval", "scope")


class Prog:
    COMPUTE = ("pe", "act", "dve", "pool")

    def __init__(self, nc, dma_depth=12):
        self.nc = nc
        self.ops = []
        self.eng = {"pe": nc.tensor, "act": nc.scalar, "dve": nc.vector, "pool": nc.gpsimd, "sp": nc.sync}
        self.tl_sem = {s: nc.alloc_semaphore("tl_" + s) for s in self.COMPUTE}
        self.Dq = dma_depth
        self.dma_sems = {q: [nc.alloc_semaphore(f"dq_{q}_{i}") for i in range(dma_depth)] for q in ("sp", "pool")}
        self.dma_ops = {"sp": [], "pool": []}
        self.last_op = {s: None for s in self.COMPUTE}
        self.waited_tl = {s: {p: -1 for p in self.COMPUTE} for s in list(self.COMPUTE) + ["sp"]}
        self.waited_dma = {s: {} for s in list(self.COMPUTE) + ["sp"]}
        self.barrier_id = 0
        self.sbuf_off = ARENA0
        self.sbuf_mark = []
        self.ntile = 0
        self.scope = None
        self.use_scopes = False

    def sb(self, shape, dtype, name=None):
        nbytes = int(np.prod(shape[1:])) * DTSIZE[dtype]
        nbytes = (nbytes + 31) // 32 * 32
        off = self.sbuf_off
        assert off + nbytes <= ARENA1, f"SBUF overflow {off}+{nbytes}"
        self.sbuf_off += nbytes
        self.ntile += 1
        h = self.nc.alloc_sbuf_tensor_at(f"{name or 't'}_{self.ntile}", list(shape), dtype, offset=off)
        return Tile(h)

    def push(self):
        self.sbuf_mark.append(self.sbuf_off)

    def pop(self):
        self.barrier()
        self.sbuf_off = self.sbuf_mark.pop()

    def _add(self, stream, is_dma, fn, reads, writes, strict=()):
        op = Op()
        op.id = len(self.ops)
        op.stream = stream
        op.is_dma = is_dma
        op.fn = fn
        op.signal = False
        op.sigval = None
        op.waits = []
        op.dsem = None
        op.scope = self.scope
        deps = set()
        for t in reads:
            b = t.buf
            if b.lw is not None:
                deps.add(b.lw)
            if t.psum:
                deps.update(b.rd)
        for t in writes:
            b = t.buf
            if b.lw is not None:
                deps.add(b.lw)
            deps.update(b.rd)
        for d in sorted(deps):
            if d < self.barrier_id:
                continue
            dop = self.ops[d]
            self._wait_on(op, dop)
        if strict is True:
            strict = list(reads) + list(writes)
        for t in strict:
            d = t.buf.lw
            if d is not None and d >= self.barrier_id and self.ops[d].stream == stream and not self.ops[d].is_dma:
                dop = self.ops[d]
                if self.waited_tl[stream][stream] < dop.id:
                    self.waited_tl[stream][stream] = dop.id
                    dop.signal = True
                    op.waits.append(("tl", dop))
        if is_dma:
            q = stream
            lst = self.dma_ops[q]
            i = len(lst)
            op.dsem = self.dma_sems[q][i % self.Dq]
            op.dval = 16 * (i // self.Dq + 1)
            if i >= self.Dq:
                self._wait_on(op, lst[i - self.Dq])
            lst.append(op)
        else:
            self.last_op[stream] = op
        for t in reads:
            t.buf.rd.append(op.id)
        for t in writes:
            t.buf.lw = op.id
            t.buf.rd = []
        self.ops.append(op)
        return op

    def _wait_on(self, op, dop):
        s = op.stream
        if dop.is_dma:
            key = id(dop.dsem)
            if self.waited_dma[s].get(key, 0) >= dop.dval:
                return
            self.waited_dma[s][key] = dop.dval
            op.waits.append(("dma", dop))
        else:
            if dop.stream == s:
                return
            if self.waited_tl[s][dop.stream] >= dop.id:
                return
            self.waited_tl[s][dop.stream] = dop.id
            dop.signal = True
            op.waits.append(("tl", dop))

    def barrier(self):
        streams = list(self.COMPUTE) + ["sp"]
        for s in streams:
            op = Op()
            op.id = len(self.ops)
            op.stream = s
            op.is_dma = False
            op.fn = None
            op.signal = False
            op.sigval = None
            op.waits = []
            op.dsem = None
            op.scope = self.scope
            for p in self.COMPUTE:
                lo = self.last_op[p]
                if lo is not None and lo.id >= 0:
                    self._wait_on(op, lo)
            for q in ("sp", "pool"):
                for dop in self.dma_ops[q][-self.Dq:]:
                    self._wait_on(op, dop)
            self.ops.append(op)
        self.barrier_id = len(self.ops)

    def pe(self, fn, reads, writes):
        return self._add("pe", False, fn, reads, writes)

    def act(self, fn, reads, writes, strict=()):
        return self._add("act", False, fn, reads, writes, strict)

    def dve(self, fn, reads, writes, strict=()):
        return self._add("dve", False, fn, reads, writes, strict)

    def pool(self, fn, reads, writes):
        return self._add("pool", False, fn, reads, writes)

    def dma(self, q, out, in_, reads, writes):
        return self._add(q, True, lambda e: e.dma_start(out=out, in_=in_), reads, writes)

    def mm(self, out, lhsT, rhs, start, stop, reads, writes, ldw=True):
        if ldw:
            return self.pe(lambda e: e.matmul(out, lhsT=lhsT, rhs=rhs, start=start, stop=stop), reads, writes)
        nc = self.nc

        def fn(e):
            with nc.discard():
                bi = e.matmul(out, lhsT=lhsT, rhs=rhs, start=start, stop=stop)
            return e.add_instruction(dataclasses.replace(bi.ins, ldweights=False, name=nc.get_next_instruction_name()))
        return self.pe(fn, reads, writes)

    def actf(self, out, in_, func, reads, writes, bias=None, scale=None, accum_out=None, strict=()):
        kw = {}
        if bias is not None:
            kw["bias"] = bias
        if scale is not None:
            kw["scale"] = scale
        if accum_out is not None:
            kw["accum_out"] = accum_out
        return self.act(lambda e: e.activation(out=out, in_=in_, func=func, **kw), reads, writes, strict)

    def tt(self, stream, out, in0, in1, op, reads, writes, strict=()):
        return self._add(stream, False, lambda e: e.tensor_tensor(out=out, in0=in0, in1=in1, op=op), reads, writes, strict)

    def ts(self, stream, out, in0, s1, s2, op0, op1, reads, writes, strict=()):
        if s2 is None:
            return self._add(stream, False, lambda e: e.tensor_scalar(out=out, in0=in0, scalar1=s1, scalar2=None, op0=op0), reads, writes, strict)
        return self._add(stream, False, lambda e: e.tensor_scalar(out=out, in0=in0, scalar1=s1, scalar2=s2, op0=op0, op1=op1), reads, writes, strict)

    def stt(self, out, in0, scalar, in1, op0, op1, reads, writes, strict=()):
        return self.dve(lambda e: e.scalar_tensor_tensor(out=out, in0=in0, scalar=scalar, in1=in1, op0=op0, op1=op1), reads, writes, strict)

    def copy(self, stream, out, in_, reads, writes):
        if stream == "act":
            return self.act(lambda e: e.copy(out=out, in_=in_), reads, writes)
        return self._add(stream, False, lambda e: e.tensor_copy(out=out, in_=in_), reads, writes)

    def emit(self):
        cnt = {s: 0 for s in self.COMPUTE}
        for op in self.ops:
            if op.signal:
                cnt[op.stream] += 1
                op.sigval = cnt[op.stream]
        cur = None
        ctx = None
        for op in self.ops:
            if self.use_scopes and op.scope != cur:
                if ctx is not None:
                    ctx.__exit__(None, None, None)
                    ctx = None
                cur = op.scope
                if cur is not None:
                    ctx = self.nc.named_scope(cur)
                    ctx.__enter__()
            e = self.eng[op.stream]
            for kind, dop in op.waits:
                if kind == "dma":
                    e.wait_ge(dop.dsem, dop.dval)
                else:
                    e.wait_ge(self.tl_sem[dop.stream], dop.sigval)
            if op.fn is None:
                continue
            ins = op.fn(e)
            if op.is_dma:
                ins.then_inc(op.dsem, 16)
            elif op.signal:
                ins.then_inc(self.tl_sem[op.stream], 1)
        if ctx is not None:
            ctx.__exit__(None, None, None)


def rev(ap):
    apl = [list(p) for p in ap.ap]
    step, cnt = apl[-1]
    off = ap.offset + step * (cnt - 1)
    apl[-1] = [-step, cnt]
    return bass.AP(ap.tensor, off, apl)


def build_nc(debug=None, stop_after=None, scopes=False):
    debug = debug or set()
    nc = bass.Bass("TRN2", target_bir_lowering=False)

    def din(name, shape, dt=F32):
        return nc.dram_tensor(name, list(shape), dt, kind="ExternalInput").ap()

    def dscr(name, shape, dt):
        kind = "ExternalOutput" if name in debug else "Internal"
        return nc.dram_tensor(name, list(shape), dt, kind=kind).ap()

    x_in = din("x", [SEQ, D])
    ctx_in = din("ctx", [CTX, D])
    cond_in = din("cond", [128, 8, 2])
    ada_w = din("ada_w", [DEPTH, D, 6 * D])
    ada_b_in = din("ada_b_fm", [128, DEPTH, 48])
    w_in = din("w_in", [DEPTH, D, 6 * D])
    convw_in = din("conv_w_fm", [128, DEPTH, 8, 4])
    convb_in = din("conv_b_fm", [128, DEPTH, 8])
    rnn_wr = din("rnn_wr", [DEPTH, 2, 8, 128, 128])
    rnn_wi = din("rnn_wi", [DEPTH, 2, 8, 128, 128])
    br_in = din("br_fm", [128, 32])
    bi_in = din("bi_fm", [128, 32])
    lam_in = din("lam_fm", [128, 32])
    attn_lam_in = din("attn_lambda", [DEPTH, 256])
    w_branch = din("w_branch", [DEPTH, 3 * D, D])
    w_gate = din("w_gate", [DEPTH, D, 3 * D])
    bgate_in = din("b_gate_fm", [128, DEPTH, 24])
    w_out = din("w_out", [DEPTH, D, D])
    ffn_w1 = din("ffn_w1", [DEPTH, D, DFF])
    ffn_w3 = din("ffn_w3", [DEPTH, D, DFF])
    ffn_w2 = din("ffn_w2", [DEPTH, DFF, D])
    fing_in = din("final_g_fm", [128, 8])
    dftL_in = din("dftL", [2, SEQ, SEQ], BF16)
    dftC_in = din("dftC", [CTX, 512], BF16)
    d128_in = din("d128", [128, 256], BF16)
    rope_in = din("ropeCS", [128, 2, SEQ], BF16)
    rotm_in = din("rotm", [128, 128], BF16)
    identf_in = din("identf", [128, 128])
    identb_in = din("identb", [128, 128], BF16)

    out = nc.dram_tensor("out", [SEQ, D], F32, kind="ExternalOutput").ap()
    xS = dscr("xS", [128, 8, T], F32)
    yaS = dscr("yaS", [128, 8, T], BF16)
    ybS = dscr("ybS", [128, 8, T], BF16)
    ycS = dscr("ycS", [128, 8, T], BF16)
    mixS = dscr("mixS", [128, 8, T], BF16)
    dbg_h = dscr("dbg_h", [128, 8, T], BF16) if "dbg_h" in debug else None
    dbg_mods = dscr("dbg_mods", [128, DEPTH, 48, 2], F32) if "dbg_mods" in debug else None

    P = Prog(nc)
    P.use_scopes = scopes
    P.scope = "setup"
    ps = [Tile(nc.alloc_psum_tensor(f"psum{i}", [128, 512], F32), psum=True) for i in range(8)]
    bank_rr = [0]

    def take(n):
        r = [ps[(bank_rr[0] + i) % 8] for i in range(n)]
        bank_rr[0] += n
        return r

    def group_mm(banks, cis, nk, lhs_fn, rhs_fn, reads_w, reads_x):
        for k in range(nk):
            for j, ci in enumerate(cis):
                w = CB[ci][1]
                P.mm(banks[j].h[:, 0:w], lhs_fn(k), rhs_fn(k, ci), k == 0, k == nk - 1, [reads_w, reads_x], [banks[j]], ldw=(j == 0))

    identf = P.sb([128, 128], F32, "identf")
    identb = P.sb([128, 128], BF16, "identb")
    ones_bf = P.sb([128, 128], BF16, "ones")
    rotm = P.sb([128, 128], BF16, "rotm")
    d128 = P.sb([128, 256], BF16, "d128")
    dC = P.sb([128, 2, 512], BF16, "dC")
    mods = P.sb([128, DEPTH, 48, 2], F32, "mods")
    convw = P.sb([128, DEPTH, 8, 4], F32, "convw")
    convb = P.sb([128, DEPTH, 8], F32, "convb")
    br_t = P.sb([128, 32], F32, "br")
    bi_t = P.sb([128, 32], F32, "bi")
    lam_t = P.sb([128, 32], F32, "lam")
    c1_t = P.sb([128, 32], F32, "c1")
    c2_t = P.sb([128, 32], F32, "c2")
    bgate = P.sb([128, DEPTH, 24], F32, "bgate")
    fing = P.sb([128, 8], F32, "fing")
    eps_t = P.sb([128, 1], F32, "eps")
    one_t = P.sb([128, 1], F32, "one")
    nlam = P.sb([128, DEPTH], F32, "nlam")
    hT = P.sb([128, 8, T], BF16, "hT")

    for dst, src in [(identf, identf_in), (identb, identb_in), (rotm, rotm_in), (d128, d128_in),
                     (convw, convw_in), (convb, convb_in), (br_t, br_in), (bi_t, bi_in), (lam_t, lam_in),
                     (bgate, bgate_in), (fing, fing_in)]:
        P.dma("sp", dst.h[:], src, [], [dst])
    P.dma("sp", dC.h[:], dftC_in.rearrange("(a p) k -> p a k", p=128), [], [dC])
    P.pool(lambda e: e.memset(ones_bf.h[:], 1.0), [], [ones_bf])
    P.pool(lambda e: e.memset(eps_t.h[:], EPS), [], [eps_t])
    P.pool(lambda e: e.memset(one_t.h[:], 1.0), [], [one_t])

    P.actf(c1_t.h[:], lam_t.h[:], AF.Exp, [lam_t], [c1_t], scale=-1.0)
    P.actf(c1_t.h[:], c1_t.h[:], AF.Ln, [c1_t, one_t], [c1_t], bias=one_t.h[:], strict=True)
    P.ts("dve", c2_t.h[:], c1_t.h[:], -16.0, None, ALU.mult, None, [c1_t], [c2_t])
    P.ts("dve", c1_t.h[:], c1_t.h[:], -8.0, None, ALU.mult, None, [c1_t], [c1_t], strict=True)

    P.push()
    al = P.sb([1, DEPTH, 256], F32, "al")
    al2 = P.sb([1, DEPTH, 2, 64], F32, "al2")
    al3 = P.sb([1, DEPTH, 2], F32, "al3")
    al4 = P.sb([1, DEPTH], F32, "al4")
    onesf = P.sb([1, 128], F32, "onesf")
    P.pool(lambda e: e.memset(onesf.h[:], 1.0), [], [onesf])
    P.dma("sp", al.h[:], attn_lam_in.rearrange("(o l) f -> o l f", o=1), [], [al])
    for l in range(DEPTH):
        lam_init = 0.8 - 0.6 * math.exp(-0.3 * l)
        for j in range(2):
            a0 = al.h[:, l, (2 * j) * 64:(2 * j + 1) * 64]
            a1 = al.h[:, l, (2 * j + 1) * 64:(2 * j + 2) * 64]
            P.tt("dve", al2.h[:, l, j, :], a0, a1, ALU.mult, [al], [al2], strict=True)
            P.dve(lambda e, o=al3.h[:, l, j:j + 1], i=al2.h[:, l, j, :]: e.reduce_sum(out=o, in_=i, axis=mybir.AxisListType.X), [al2], [al3], strict=True)
        P.actf(al3.h[:, l, :], al3.h[:, l, :], AF.Exp, [al3], [al3], strict=True)
        P.tt("dve", al4.h[:, l:l + 1], al3.h[:, l, 1:2], al3.h[:, l, 0:1], ALU.subtract, [al3], [al4], strict=True)
        P.ts("dve", al4.h[:, l:l + 1], al4.h[:, l:l + 1], -lam_init, None, ALU.add, None, [al4], [al4], strict=True)
    P.mm(ps[0].h[:, 0:DEPTH], onesf.h[:], al4.h[:], True, True, [onesf, al4], [ps[0]])
    P.copy("dve", nlam.h[:], ps[0].h[:, 0:DEPTH], [ps[0]], [nlam])
    P.pop()

    P.scope = "adaln"
    P.push()
    cond = P.sb([128, 8, 2], F32, "cond")
    condr = P.sb([128, 8, 2], F32R, "condr")
    adab = P.sb([128, DEPTH, 48], F32, "adab")
    wA = [P.sb([128, 8, 512], F32R, "wA") for _ in range(2)]
    P.dma("sp", cond.h[:], cond_in, [], [cond])
    P.dma("sp", adab.h[:], ada_b_in, [], [adab])
    P.actf(condr.h[:], cond.h[:], AF.Silu, [cond], [condr])
    nblk = 0
    for l in range(DEPTH):
        pm = ps[l]
        for blk in range(12):
            w = wA[nblk % 2]
            nblk += 1
            P.dma("pool", w.h[:], ada_w[l][:, blk * 512:(blk + 1) * 512].rearrange("(k p) f -> p k f", p=128), [], [w])
            for j in range(4):
                oc = blk * 4 + j
                for k in range(8):
                    P.mm(pm.h[:, oc * 2:oc * 2 + 2], w.h[:, k, j * 128:(j + 1) * 128], condr.h[:, k, :], k == 0, k == 7, [w, condr], [pm])
        pv = pm.h[:, 0:96].rearrange("p (a b) -> p a b", b=2)
        for j in range(2):
            P.tt("dve", mods.h[:, l, :, j], pv[:, :, j], adab.h[:, l, :], ALU.add, [pm, adab], [mods], strict=True)
        for m in (1, 4):
            P.ts("dve", mods.h[:, l, m * 8:(m + 1) * 8, :], mods.h[:, l, m * 8:(m + 1) * 8, :], 1.0, None, ALU.add, None, [mods], [mods], strict=True)
    if dbg_mods is not None:
        P.dma("sp", dbg_mods, mods.h[:], [mods], [])
    P.pop()

    def mod(l, m, k, j):
        return mods.h[:, l, m * 8 + k, j:j + 1]

    P.scope = "stage0"
    P.push()
    xin = [P.sb([128, D], F32, "xin") for _ in range(2)]
    xst = [P.sb([128, 8, 512], F32, "xst") for _ in range(2)]
    nt = 0
    for ci, (c0, w) in enumerate(CB):
        st = xst[ci % 2]
        for t4 in range(w // 128):
            tok0 = c0 + t4 * 128
            xi = xin[nt % 2]
            src = ctx_in[tok0:tok0 + 128, :] if tok0 < CTX else x_in[tok0 - CTX:tok0 - CTX + 128, :]
            P.dma("sp", xi.h[:], src, [], [xi])
            for half in range(2):
                pt = ps[(nt * 2 + half) % 4]
                for kk in range(4):
                    k = half * 4 + kk
                    P.pe(lambda e, o=pt.h[:, kk * 128:(kk + 1) * 128], i=xi.h[:, k * 128:(k + 1) * 128]: e.transpose(o, i, identf.h[:]), [xi, identf], [pt])
                P.copy("act" if half == 0 else "dve", st.h[:, half * 4:(half + 1) * 4, t4 * 128:(t4 + 1) * 128],
                       pt.h[:, 0:512].rearrange("p (a b) -> p a b", b=128), [pt], [st])
            nt += 1
        P.dma("sp", xS[:, :, c0:c0 + w], st.h[:, :, 0:w], [st], [])
    P.pop()

    def norm_block(l, sub, ci, xb, sq, rt, rstd, tmp):
        c0, w = CB[ci]
        j = 1 if ci == 0 else 0
        msh, msc = (0, 1) if sub == 1 else (3, 4)
        pss = ps[4 + (ci % 2)]
        P.actf(sq.h[:, :, 0:w], xb.h[:, :, 0:w], AF.Square, [xb], [sq])
        for k in range(8):
            P.mm(pss.h[:, 0:w], ones_bf.h[:], sq.h[:, k, 0:w], k == 0, k == 7, [ones_bf, sq], [pss])
        P.actf(rt.h[:, 0:w], pss.h[:, 0:w], AF.Sqrt, [pss, eps_t], [rt], bias=eps_t.h[:], scale=1.0 / D)
        P.dve(lambda e: e.reciprocal(out=rstd.h[:, 0:w], in_=rt.h[:, 0:w]), [rt], [rstd])
        for k in range(8):
            tm = tmp[k % 2]
            P.tt("dve", tm.h[:, 0:w], xb.h[:, k, 0:w], rstd.h[:, 0:w], ALU.mult, [xb, rstd], [tm])
            P.actf(hT.h[:, k, c0:c0 + w], tm.h[:, 0:w], AF.Identity, [tm, mods], [hT], bias=mod(l, msh, k, j), scale=mod(l, msc, k, j))

    def norm_tiles():
        sq = P.sb([128, 8, 512], BF16, "sq")
        rt = P.sb([128, 512], F32, "rt")
        rstd = P.sb([128, 512], F32, "rstd")
        tmp = [P.sb([128, 512], F32, "ntmp") for _ in range(2)]
        return sq, rt, rstd, tmp

    def proj_fm(wt, wsl, ci, pst):
        c0, w = CB[ci]
        for k in range(8):
            P.mm(pst.h[:, 0:w], wt.h[:, k, wsl], hT.h[:, k, c0:c0 + w], k == 0, k == 7, [wt, hT], [pst])

    nlayers = DEPTH
    for l in range(nlayers):
        last = l == DEPTH - 1
        lam_init = 0.8 - 0.6 * math.exp(-0.3 * l)
        cbs = list(range(1, 5)) if last else list(range(5))

        P.scope = f"L{l}_n1"
        P.push()
        xb = [P.sb([128, 8, 512], F32, "xb") for _ in range(2)]
        nt_ = norm_tiles()
        for ci, (c0, w) in enumerate(CB):
            b = xb[ci % 2]
            P.dma("sp", b.h[:, :, 0:w], xS[:, :, c0:c0 + w], [], [b])
            norm_block(l, 1, ci, b, *nt_)
        if dbg_h is not None and l == 0:
            P.dma("sp", dbg_h, hT.h[:], [hT], [])
        P.pop()
        if stop_after == "n1":
            break

        P.scope = f"L{l}_rnn"
        P.push()
        xrp2 = [P.sb([128, GW + 3], F32, "xrp") for _ in range(2)]
        u2 = [P.sb([128, GW], F32, "u") for _ in range(2)]
        A2 = [P.sb([128, GW], F32, "A") for _ in range(2)]
        M2 = [P.sb([128, GW], F32, "M") for _ in range(2)]
        I2 = [P.sb([128, GW], F32, "I") for _ in range(2)]
        H = [P.sb([128, GW], F32, "H") for _ in range(2)]
        ub2 = [P.sb([128, GW], BF16, "ub") for _ in range(2)]
        gg2 = [P.sb([128, GW], BF16, "gg") for _ in range(2)]
        yat = [P.sb([128, GW], BF16, "yat") for _ in range(2)]
        wxg = [P.sb([128, 8, 256], BF16, "wxg") for _ in range(2)]
        wri = P.sb([128, 4, 8, 128], BF16, "wri")
        for t_ in xrp2 + gg2:
            P.pool(lambda e, t_=t_: e.memset(t_.h[:], 0.0), [], [t_])
        for t_ in H:
            P.pool(lambda e, t_=t_: e.memset(t_.h[:], 0.0), [], [t_])
        for d in range(2):
            P.dma("pool", wri.h[:, d * 2 + 0, :, :], rnn_wr[l, d].rearrange("g i j -> i g j"), [], [wri])
            P.dma("pool", wri.h[:, d * 2 + 1, :, :], rnn_wi[l, d].rearrange("g i j -> i g j"), [], [wri])
        npp = 0
        for g in range(8):
            xrp, u, ub, gg = xrp2[g % 2], u2[g % 2], ub2[g % 2], gg2[g % 2]
            wt = wxg[g % 2]
            P.dma("pool", wt.h[:, :, 0:128], w_in[l][:, g * 128:(g + 1) * 128].rearrange("(k p) f -> p k f", p=128), [], [wt])
            P.dma("pool", wt.h[:, :, 128:256], w_in[l][:, D + g * 128:D + (g + 1) * 128].rearrange("(k p) f -> p k f", p=128), [], [wt])
            for ci, (c0, w) in enumerate(CB):
                go = 0 if ci == 0 else 259 + (c0 - 256)
                pst = ps[npp % 4]; npp += 1
                proj_fm(wt, slice(0, 128), ci, pst)
                P.copy("act", xrp.h[:, go + 2:go + 2 + w], pst.h[:, 0:w], [pst], [xrp])
            for ci, (c0, w) in enumerate(CB):
                go = 0 if ci == 0 else 259 + (c0 - 256)
                pst = ps[npp % 4]; npp += 1
                proj_fm(wt, slice(128, 256), ci, pst)
                P.actf(gg.h[:, go:go + w], pst.h[:, 0:w], AF.Gelu, [pst], [gg])
            P.ts("dve", u.h[:], xrp.h[:, 0:GW], convw.h[:, l, g, 0:1], convb.h[:, l, g:g + 1], ALU.mult, ALU.add, [xrp, convw, convb], [u])
            for k in range(1, 4):
                P.stt(u.h[:], xrp.h[:, k:k + GW], convw.h[:, l, g, k:k + 1], u.h[:], ALU.mult, ALU.add, [xrp, u, convw], [u])
            P.copy("pool", ub.h[:], u.h[:], [u], [ub])
            for d in range(2):
                A, I = A2[d], I2[d]
                ix = (l * 2 + d) * 8 + g
                for (gc0, gw_) in GB:
                    pst = ps[npp % 4]; npp += 1
                    P.mm(pst.h[:, 0:gw_], wri.h[:, d * 2, g, :], ub.h[:, gc0:gc0 + gw_], True, True, [wri, ub], [pst])
                    P.actf(A.h[:, gc0:gc0 + gw_], pst.h[:, 0:gw_], AF.Sigmoid, [pst, br_t], [A], bias=br_t.h[:, ix:ix + 1])
                for (gc0, gw_) in GB:
                    pst = ps[npp % 4]; npp += 1
                    P.mm(pst.h[:, 0:gw_], wri.h[:, d * 2 + 1, g, :], ub.h[:, gc0:gc0 + gw_], True, True, [wri, ub], [pst])
                    P.actf(I.h[:, gc0:gc0 + gw_], pst.h[:, 0:gw_], AF.Sigmoid, [pst, bi_t], [I], bias=bi_t.h[:, ix:ix + 1])
                P.tt("pool", I.h[:], I.h[:], u.h[:], ALU.mult, [I, u], [I])
            for d in range(2):
                A, M = A2[d], M2[d]
                ix = (l * 2 + d) * 8 + g
                P.actf(M.h[:], A.h[:], AF.Exp, [A, c2_t], [M], scale=c2_t.h[:, ix:ix + 1])
                P.actf(A.h[:], A.h[:], AF.Exp, [A, c1_t], [A], scale=c1_t.h[:, ix:ix + 1])
            for d in range(2):
                M = M2[d]
                P.actf(M.h[:], M.h[:], AF.Sqrt, [M, one_t], [M], bias=one_t.h[:], scale=-1.0)
            for d in range(2):
                A, M, I = A2[d], M2[d], I2[d]
                P.tt("dve", I.h[:], I.h[:], M.h[:], ALU.mult, [I, M], [I])
                Hd = H[d]
                if d == 0:
                    P.dve(lambda e, o=Hd.h[:, 0:256], a=A.h[:, 0:256], b=I.h[:, 0:256]: e.tensor_tensor_scan(out=o, data0=a, data1=b, initial=0.0, op0=ALU.mult, op1=ALU.add), [A, I], [Hd])
                    P.dve(lambda e, o=Hd.h[:, 259:GW], a=A.h[:, 259:GW], b=I.h[:, 259:GW], i0=Hd.h[:, 255:256]: e.tensor_tensor_scan(out=o, data0=a, data1=b, initial=i0, op0=ALU.mult, op1=ALU.add), [A, I, Hd], [Hd], strict=[Hd])
                else:
                    P.dve(lambda e, o=rev(Hd.h[:, 0:256]), a=rev(A.h[:, 0:256]), b=rev(I.h[:, 0:256]): e.tensor_tensor_scan(out=o, data0=a, data1=b, initial=0.0, op0=ALU.mult, op1=ALU.add), [A, I], [Hd])
                    P.dve(lambda e, o=rev(Hd.h[:, 259:GW]), a=rev(A.h[:, 259:GW]), b=rev(I.h[:, 259:GW]), i0=Hd.h[:, 0:1]: e.tensor_tensor_scan(out=o, data0=a, data1=b, initial=i0, op0=ALU.mult, op1=ALU.add), [A, I, Hd], [Hd], strict=[Hd])
            yt = yat[g % 2]
            P.tt("dve", H[0].h[:], H[0].h[:], H[1].h[:], ALU.add, [H[0], H[1]], [H[0]])
            P.tt("dve", yt.h[:], H[0].h[:], gg.h[:], ALU.mult, [H[0], gg], [yt])
            if not last:
                P.dma("sp", yaS[:, g, 0:256], yt.h[:, 0:256], [yt], [])
            P.dma("sp", yaS[:, g, 256:T], yt.h[:, 259:GW], [yt], [])
        P.pop()
        if stop_after == "rnn":
            break

        P.scope = f"L{l}_fourier"
        P.push()
        xf = P.sb([128, 18, D], BF16, "xf")
        wxf = [P.sb([128, 8, 512], BF16, "wxf") for _ in range(2)]
        Dt = [P.sb([128, 16, 512], BF16, "Dt") for _ in range(2)]
        Pcs = P.sb([128, 2, 8, 512], BF16, "Pcs")
        ybb = [P.sb([128, 8, 512], BF16, "ybb") for _ in range(2)]
        npp = 0
        tts = list(range(2, 18)) if last else list(range(18))
        for half in range(2):
            wt = wxf[half]
            P.dma("pool", wt.h[:], w_in[l][:, 2 * D + half * 512:2 * D + (half + 1) * 512].rearrange("(k p) f -> p k f", p=128), [], [wt])
            for tt_ in tts:
                pst = ps[npp % 4]; npp += 1
                for k in range(8):
                    P.mm(pst.h[:, 0:512], hT.h[:, k, tt_ * 128:(tt_ + 1) * 128], wt.h[:, k, :], k == 0, k == 7, [hT, wt], [pst])
                P.copy("act" if npp % 2 else "dve", xf.h[:, tt_, half * 512:(half + 1) * 512], pst.h[:, 0:512], [pst], [xf])
        nd = 0
        for kb in range(4):
            for cs in range(2):
                dt_ = Dt[nd % 2]; nd += 1
                P.dma("sp", dt_.h[:], dftL_in[cs][:, kb * 512:(kb + 1) * 512].rearrange("(a p) k -> p a k", p=128), [], [dt_])
                for g in range(8):
                    pst = ps[npp % 4]; npp += 1
                    for a in range(16):
                        P.mm(pst.h[:, 0:512], xf.h[:, 2 + a, g * 128:(g + 1) * 128], dt_.h[:, a, :], a == 0, a == 15, [xf, dt_], [pst])
                    P.copy("act" if g % 2 else "dve", Pcs.h[:, cs, g, :], pst.h[:, 0:512], [pst], [Pcs])
            yb_ = ybb[kb % 2]
            for g in range(8):
                pst = ps[4 + npp % 2]; npp += 1
                P.mm(pst.h[:, 0:512], d128.h[:, 0:128], Pcs.h[:, 0, g, :], True, False, [d128, Pcs], [pst])
                P.mm(pst.h[:, 0:512], d128.h[:, 128:256], Pcs.h[:, 1, g, :], False, True, [d128, Pcs], [pst])
                P.copy("act" if g % 2 else "dve", yb_.h[:, g, :], pst.h[:, 0:512], [pst], [yb_])
            P.dma("sp", ybS[:, :, 256 + kb * 512:256 + (kb + 1) * 512], yb_.h[:], [yb_], [])
        if not last:
            ybc = P.sb([128, 8, 256], BF16, "ybc")
            Pc = [P.sb([128, 512], BF16, "Pc") for _ in range(2)]
            for g in range(8):
                pst = ps[npp % 4]; npp += 1
                pc_ = Pc[g % 2]
                for a in range(2):
                    P.mm(pst.h[:, 0:512], xf.h[:, a, g * 128:(g + 1) * 128], dC.h[:, a, :], a == 0, a == 1, [xf, dC], [pst])
                P.copy("act", pc_.h[:], pst.h[:, 0:512], [pst], [pc_])
                pst2 = ps[4 + npp % 2]; npp += 1
                P.mm(pst2.h[:, 0:256], d128.h[:, 0:128], pc_.h[:, 0:256], True, False, [d128, pc_], [pst2])
                P.mm(pst2.h[:, 0:256], d128.h[:, 128:256], pc_.h[:, 256:512], False, True, [d128, pc_], [pst2])
                P.copy("dve", ybc.h[:, g, :], pst2.h[:, 0:256], [pst2], [ybc])
            P.dma("sp", ybS[:, :, 0:256], ybc.h[:], [ybc], [])
        P.pop()
        if stop_after == "fourier":
            break

        P.scope = f"L{l}_attn"
        P.push()
        vt = P.sb([128, 18, 8, 130], BF16, "vt")
        rope = P.sb([128, 2, SEQ], BF16, "rope")
        tb = [P.sb([128, 512], BF16, "tb") for _ in range(2)]
        tc_ = [P.sb([128, 512], F32, "tc") for _ in range(2)]
        ts_ = [P.sb([128, 512], F32, "ts") for _ in range(2)]
        eT = [P.sb([128, 18, 512], BF16, "eT") for _ in range(2)]
        obt = [P.sb([128, 4, 2, 130], F32, "obt") for _ in range(2)]
        s16 = [P.sb([128, 16], F32, "s16") for _ in range(2)]
        o2t = [P.sb([128, 4, 128], F32, "o2t") for _ in range(2)]
        t1t = [P.sb([128, 4, 128], F32, "t1t") for _ in range(2)]
        yc4 = [P.sb([128, 4, 128], BF16, "yc4") for _ in range(2)]
        kk_ = 1.0 - lam_init
        epsk = P.sb([128, 1], F32, "epsk")
        P.pool(lambda e: e.memset(epsk.h[:], EPS / (kk_ * kk_)), [], [epsk])
        ycT = [P.sb([128, T], BF16, "ycT") for _ in range(2)]
        P.dma("sp", rope.h[:], rope_in, [], [rope])
        P.pool(lambda e: e.memset(vt.h[:], 1.0), [], [vt])
        npp = 0
        P.push()
        wv = [P.sb([128, 8, 512], BF16, "wv") for _ in range(2)]
        for half in range(2):
            wt = wv[half]
            P.dma("pool", wt.h[:], w_in[l][:, 5 * D + half * 512:5 * D + (half + 1) * 512].rearrange("(k p) f -> p k f", p=128), [], [wt])
            for tt_ in range(18):
                pst = ps[npp % 4]; npp += 1
                for k in range(8):
                    P.mm(pst.h[:, 0:512], hT.h[:, k, tt_ * 128:(tt_ + 1) * 128], wt.h[:, k, :], k == 0, k == 7, [hT, wt], [pst])
                P.copy("act" if npp % 2 else "dve", vt.h[:, tt_, half * 4:(half + 1) * 4, 0:128],
                       pst.h[:, 0:512].rearrange("p (a b) -> p a b", b=128), [pst], [vt])
        P.pop()
        wqk = [P.sb([128, 8, 256], BF16, "wqk") for _ in range(2)]
        qz = [[P.sb([128, T], BF16, "qz") for _ in range(2)] for _ in range(2)]
        kT2 = [P.sb([128, T], BF16, "kT") for _ in range(2)]
        for a_ in range(2):
            for b_ in range(2):
                P.pool(lambda e, t_=qz[a_][b_]: e.memset(t_.h[:], 0.0), [], [qz[a_][b_]])
        nrr = [0]
        nq = [0]
        nppa = [npp]

        def nextps():
            t_ = ps[nppa[0] % 4]
            nppa[0] += 1
            return t_

        def emit_proj(h):
            wt = wqk[h % 2]
            P.dma("pool", wt.h[:, :, 0:128], w_in[l][:, 3 * D + h * 128:3 * D + (h + 1) * 128].rearrange("(k p) f -> p k f", p=128), [], [wt])
            P.dma("pool", wt.h[:, :, 128:256], w_in[l][:, 4 * D + h * 128:4 * D + (h + 1) * 128].rearrange("(k p) f -> p k f", p=128), [], [wt])
            for which in (0, 1):
                for ci, (c0, w) in enumerate(CB):
                    if ci == 0 and which == 0 and last:
                        continue
                    pst = nextps()
                    proj_fm(wt, slice(which * 128, (which + 1) * 128), ci, pst)
                    if which == 0:
                        dsts = [(qz[h % 2][c], slice(c * 64, (c + 1) * 64)) for c in range(2)]
                    else:
                        dsts = [(kT2[h % 2], slice(0, 128))]
                    if ci == 0:
                        for dt_, sl in dsts:
                            P.copy("act", dt_.h[sl, 0:256], pst.h[sl, 0:256], [pst], [dt_])
                    else:
                        r0 = c0 - 256
                        tb_, tcc, tss = tb[nrr[0] % 2], tc_[nrr[0] % 2], ts_[nrr[0] % 2]
                        nrr[0] += 1
                        P.copy("act", tb_.h[:], pst.h[:, 0:512], [pst], [tb_])
                        P.tt("dve", tcc.h[:], pst.h[:, 0:512], rope.h[:, 0, r0:r0 + 512], ALU.mult, [pst, rope], [tcc])
                        pst2 = nextps()
                        P.mm(pst2.h[:, 0:512], rotm.h[:], tb_.h[:], True, True, [rotm, tb_], [pst2])
                        P.tt("dve", tss.h[:], pst2.h[:, 0:512], rope.h[:, 1, r0:r0 + 512], ALU.mult, [pst2, rope], [tss])
                        for dt_, sl in dsts:
                            P.tt("dve", dt_.h[sl, c0:c0 + 512], tcc.h[sl, :], tss.h[sl, :], ALU.add, [tcc, tss], [dt_])

        units = [(h, ci, c) for h in range(8) for ci in cbs for c in range(2)]

        def gen_A(i):
            h, ci, c = units[i]
            if i == 0:
                emit_proj(0)
            if ci == cbs[1] and c == 0 and h + 1 < 8:
                emit_proj(h + 1)
            c0, w = CB[ci]
            kts = [0, 1] if ci == 0 else list(range(18))
            e_ = eT[i % 2]
            qT, kT = qz[h % 2][c], kT2[h % 2]
            for kt in kts:
                pst = nextps()
                P.mm(pst.h[:, 0:w], kT.h[:, kt * 128:(kt + 1) * 128], qT.h[:, c0:c0 + w], True, True, [kT, qT], [pst])
                P.actf(e_.h[:, kt, 0:w], pst.h[:, 0:w], AF.Exp, [pst], [e_], scale=0.125)
                yield

        def bc(ap2, n):
            apl = [list(p_) for p_ in ap2.ap]
            return bass.AP(ap2.tensor, ap2.offset, apl + [[0, n]])

        def norm_ci(h, ci):
            c0, w = CB[ci]
            nqt = w // 128
            yh = ycT[h % 2]
            ob = obt[ci % 2]
            k_ = nq[0] % 2
            nq[0] += 1
            s_ = s16[k_]; o2_ = o2t[k_]; t1_ = t1t[k_]; yc_ = yc4[k_]
            rzv = s_.h[:, 0:2 * nqt].rearrange("p (a b) -> p a b", b=2)
            nl = s_.h[:, 8:8 + nqt]
            ssv = s_.h[:, 12:12 + nqt]
            P.dve(lambda e: e.reciprocal(out=rzv, in_=ob.h[:, 0:nqt, :, 128]), [ob], [s_], strict=True)
            P.ts("dve", nl, rzv[:, :, 1], nlam.h[:, l:l + 1], None, ALU.mult, None, [s_, nlam], [s_], strict=True)
            P.tt("dve", t1_.h[:, 0:nqt, :], ob.h[:, 0:nqt, 1, 0:128], bc(nl, 128), ALU.mult, [ob, s_], [t1_], strict=True)
            P.tt("dve", o2_.h[:, 0:nqt, :], ob.h[:, 0:nqt, 0, 0:128], bc(rzv[:, :, 0], 128), ALU.mult, [ob, s_], [o2_], strict=True)
            P.tt("dve", o2_.h[:, 0:nqt, :], o2_.h[:, 0:nqt, :], t1_.h[:, 0:nqt, :], ALU.add, [o2_, t1_], [o2_], strict=True)
            P.tt("dve", t1_.h[:, 0:nqt, :], o2_.h[:, 0:nqt, :], o2_.h[:, 0:nqt, :], ALU.mult, [o2_], [t1_], strict=True)
            P.dve(lambda e: e.reduce_sum(out=ssv, in_=t1_.h[:, 0:nqt, :], axis=mybir.AxisListType.X), [t1_], [s_], strict=True)
            P.actf(ssv, ssv, AF.Sqrt, [s_, epsk], [s_], bias=epsk.h[:], scale=1.0 / (128 * kk_ * kk_), strict=True)
            P.dve(lambda e: e.reciprocal(out=ssv, in_=ssv), [s_], [s_], strict=True)
            P.tt("dve", yc_.h[:, 0:nqt, :], o2_.h[:, 0:nqt, :], bc(ssv, 128), ALU.mult, [o2_, s_], [yc_], strict=True)
            pst = nextps()
            pv = pst.h[:, 0:256].bitcast(BF16)
            for qt in range(nqt):
                P.pe(lambda e, o=pv[:, qt * 128:(qt + 1) * 128], i=yc_.h[:, qt, :]: e.transpose(o, i, identb.h[:]), [yc_, identb], [pst])
            P.copy("dve", yh.h[:, c0:c0 + w], pv[:, 0:w], [pst], [yh])

        def gen_B(i):
            h, ci, c = units[i]
            c0, w = CB[ci]
            kts = [0, 1] if ci == 0 else list(range(18))
            nqt = w // 128
            e_ = eT[i % 2]
            for i_, kt in enumerate(kts):
                for qt in range(nqt):
                    pso = ps[4 + qt]
                    P.mm(pso.h[:, 0:130], e_.h[:, kt, qt * 128:(qt + 1) * 128], vt.h[:, kt, h, :], i_ == 0, i_ == len(kts) - 1, [e_, vt], [pso])
                yield
            for qt in range(nqt):
                P.copy("dve", obt[ci % 2].h[:, qt, c, :], ps[4 + qt].h[:, 0:130], [ps[4 + qt]], [obt[ci % 2]])
            if c == 1:
                norm_ci(h, ci)
                if ci == cbs[-1]:
                    yh = ycT[h % 2]
                    if not last:
                        P.dma("sp", ycS[:, h, 0:256], yh.h[:, 0:256], [yh], [])
                    P.dma("sp", ycS[:, h, 256:T], yh.h[:, 256:T], [yh], [])
            yield

        nun_ = len(units)
        if stop_after in ("attn_v",):
            nun_ = 0
        gb = iter(())
        for i in range(nun_ + 1):
            ga = gen_A(i) if i < nun_ else iter(())
            da = db = False
            while not (da and db):
                if not da:
                    try:
                        next(ga)
                    except StopIteration:
                        da = True
                if not db:
                    try:
                        next(gb)
                    except StopIteration:
                        db = True
            gb = gen_B(i) if i < nun_ else iter(())
        P.pop()
        if stop_after is not None and stop_after.startswith("attn"):
            break

        P.scope = f"L{l}_merge"
        P.push()
        yT3 = [P.sb([128, 8, T], BF16, "yT") for _ in range(3)]
        wb = [P.sb([128, 24, 128], BF16, "wb") for _ in range(2)]
        wg = [P.sb([128, 8, 3, 128], BF16, "wg") for _ in range(2)]
        ncb = len(cbs)
        gsb = [P.sb([128, 512], F32, "gsb") for _ in range(ncb)]
        mix = [P.sb([128, 512], F32, "mix") for _ in range(ncb)]
        mtmp = [P.sb([128, 512], F32, "mtmp") for _ in range(2)]
        mixb = [P.sb([128, 512], BF16, "mixb") for _ in range(ncb)]
        lo = 256 if last else 0
        for br, src in enumerate((yaS, ybS, ycS)):
            for k in range(8):
                P.dma("sp", yT3[br].h[:, k, lo:T], src[:, k, lo:T], [], [yT3[br]])
        for oc in range(8):
            wbt = wb[oc % 2]; wgt = wg[oc % 2]
            P.dma("pool", wbt.h[:], w_branch[l][:, oc * 128:(oc + 1) * 128].rearrange("(k p) f -> p k f", p=128), [], [wbt])
            for br in range(3):
                P.dma("pool", wgt.h[:, :, br, :], w_gate[l][:, br * D + oc * 128:br * D + (oc + 1) * 128].rearrange("(k p) f -> p k f", p=128), [], [wgt])
            for br in range(3):
                bg = take(ncb)
                group_mm(bg, cbs, 8, lambda k: wgt.h[:, k, br, :], lambda k, ci: hT.h[:, k, CB[ci][0]:CB[ci][0] + CB[ci][1]], wgt, hT)
                for j, ci in enumerate(cbs):
                    w = CB[ci][1]
                    P.actf(gsb[j].h[:, 0:w], bg[j].h[:, 0:w], AF.Sigmoid, [bg[j], bgate], [gsb[j]], bias=bgate.h[:, l, br * 8 + oc:br * 8 + oc + 1])
                bp = take(ncb)
                yb_ = yT3[br]
                group_mm(bp, cbs, 8, lambda k: wbt.h[:, br * 8 + k, :], lambda k, ci: yb_.h[:, k, CB[ci][0]:CB[ci][0] + CB[ci][1]], wbt, yb_)
                for j, ci in enumerate(cbs):
                    w = CB[ci][1]
                    if br == 0:
                        P.tt("dve", mix[j].h[:, 0:w], bp[j].h[:, 0:w], gsb[j].h[:, 0:w], ALU.mult, [bp[j], gsb[j]], [mix[j]])
                    else:
                        mt = mtmp[j % 2]
                        P.tt("dve", mt.h[:, 0:w], bp[j].h[:, 0:w], gsb[j].h[:, 0:w], ALU.mult, [bp[j], gsb[j]], [mt])
                        dst = mix[j] if br == 1 else mixb[j]
                        P.tt("dve", dst.h[:, 0:w], mix[j].h[:, 0:w], mt.h[:, 0:w], ALU.add, [mix[j], mt], [dst])
            for j, ci in enumerate(cbs):
                c0, w = CB[ci]
                P.dma("sp", mixS[:, oc, c0:c0 + w], mixb[j].h[:, 0:w], [mixb[j]], [])
        P.pop()
        if stop_after == "merge":
            break

        P.scope = f"L{l}_m2"
        P.push()
        wo = P.sb([128, 8, D], BF16, "wo")
        mxb = [P.sb([128, 8, 512], BF16, "mxb") for _ in range(2)]
        xb = [P.sb([128, 8, 512], F32, "xb2") for _ in range(2)]
        nt_ = norm_tiles()
        P.dma("pool", wo.h[:], w_out[l].rearrange("(k p) f -> p k f", p=128), [], [wo])
        npp = 0
        for n_, ci in enumerate(cbs):
            c0, w = CB[ci]
            j = 1 if ci == 0 else 0
            mb = mxb[n_ % 2]; b = xb[n_ % 2]
            P.dma("sp", mb.h[:, :, 0:w], mixS[:, :, c0:c0 + w], [], [mb])
            P.dma("sp", b.h[:, :, 0:w], xS[:, :, c0:c0 + w], [], [b])
            for oc in range(8):
                pst = ps[npp % 4]; npp += 1
                for k in range(8):
                    P.mm(pst.h[:, 0:w], wo.h[:, k, oc * 128:(oc + 1) * 128], mb.h[:, k, 0:w], k == 0, k == 7, [wo, mb], [pst])
                P.stt(b.h[:, oc, 0:w], pst.h[:, 0:w], mod(l, 2, oc, j), b.h[:, oc, 0:w], ALU.mult, ALU.add, [pst, mods, b], [b])
            P.dma("sp", xS[:, :, c0:c0 + w], b.h[:, :, 0:w], [b], [])
            norm_block(l, 2, ci, b, *nt_)
        P.pop()
        if stop_after == "m2":
            break

        P.scope = f"L{l}_ffn"
        P.push()
        uT = P.sb([128, NFF, T], BF16, "uT")
        w13 = [P.sb([128, 8, 2, 128], BF16, "w13") for _ in range(2)]
        w2t = [P.sb([128, NFF, 128], BF16, "w2t") for _ in range(2)]
        ncb = len(cbs)
        ssb = [P.sb([128, 512], F32, "ssb") for _ in range(2 * ncb)]
        xq = [P.sb([128, 512], F32, "xq") for _ in range(ncb + 1)]
        for oc in range(NFF):
            wt = w13[oc % 2]
            P.dma("pool", wt.h[:, :, 0, :], ffn_w1[l][:, oc * 128:(oc + 1) * 128].rearrange("(k p) f -> p k f", p=128), [], [wt])
            P.dma("pool", wt.h[:, :, 1, :], ffn_w3[l][:, oc * 128:(oc + 1) * 128].rearrange("(k p) f -> p k f", p=128), [], [wt])
            b1 = take(ncb)
            group_mm(b1, cbs, 8, lambda k: wt.h[:, k, 0, :], lambda k, ci: hT.h[:, k, CB[ci][0]:CB[ci][0] + CB[ci][1]], wt, hT)
            sset = ssb[(oc % 2) * ncb:(oc % 2 + 1) * ncb]
            for j, ci in enumerate(cbs):
                w = CB[ci][1]
                P.actf(sset[j].h[:, 0:w], b1[j].h[:, 0:w], AF.Silu, [b1[j]], [sset[j]])
            b3 = take(ncb)
            group_mm(b3, cbs, 8, lambda k: wt.h[:, k, 1, :], lambda k, ci: hT.h[:, k, CB[ci][0]:CB[ci][0] + CB[ci][1]], wt, hT)
            for j, ci in enumerate(cbs):
                c0, w = CB[ci]
                P.tt("dve", uT.h[:, oc, c0:c0 + w], b3[j].h[:, 0:w], sset[j].h[:, 0:w], ALU.mult, [b3[j], sset[j]], [uT])
        nx = 0
        for oc in range(8):
            wt = w2t[oc % 2]
            P.dma("pool", wt.h[:], ffn_w2[l][:, oc * 128:(oc + 1) * 128].rearrange("(k p) f -> p k f", p=128), [], [wt])
            xqs = []
            for j, ci in enumerate(cbs):
                c0, w = CB[ci]
                xq_ = xq[nx % (ncb + 1)]; nx += 1
                P.dma("sp", xq_.h[:, 0:w], xS[:, oc, c0:c0 + w], [], [xq_])
                xqs.append(xq_)
            bb = take(ncb)
            group_mm(bb, cbs, NFF, lambda k: wt.h[:, k, :], lambda k, ci: uT.h[:, k, CB[ci][0]:CB[ci][0] + CB[ci][1]], wt, uT)
            for j, ci in enumerate(cbs):
                c0, w = CB[ci]
                jx = 1 if ci == 0 else 0
                xq_ = xqs[j]
                P.stt(xq_.h[:, 0:w], bb[j].h[:, 0:w], mod(l, 5, oc, jx), xq_.h[:, 0:w], ALU.mult, ALU.add, [bb[j], mods, xq_], [xq_])
                P.dma("sp", xS[:, oc, c0:c0 + w], xq_.h[:, 0:w], [xq_], [])
        P.pop()

    if stop_after is None:
        P.scope = "final"
        P.push()
        xb = [P.sb([128, 8, 512], F32, "xbf") for _ in range(2)]
        sq = P.sb([128, 8, 512], BF16, "sqf")
        rt = P.sb([128, 512], F32, "rtf")
        rstd = P.sb([128, 512], F32, "rstdf")
        ykall = [P.sb([128, 8, 512], F32, "ykall") for _ in range(2)]
        otile = [P.sb([128, D], F32, "otile") for _ in range(4)]
        npp = 0
        for n_, ci in enumerate(range(1, 5)):
            c0, w = CB[ci]
            b = xb[n_ % 2]
            P.dma("sp", b.h[:], xS[:, :, c0:c0 + w], [], [b])
            pss = ps[4 + n_ % 2]
            P.actf(sq.h[:], b.h[:], AF.Square, [b], [sq])
            for k in range(8):
                P.mm(pss.h[:, 0:w], ones_bf.h[:], sq.h[:, k, :], k == 0, k == 7, [ones_bf, sq], [pss])
            P.actf(rt.h[:], pss.h[:, 0:w], AF.Sqrt, [pss, eps_t], [rt], bias=eps_t.h[:], scale=1.0 / D)
            P.dve(lambda e: e.reciprocal(out=rstd.h[:], in_=rt.h[:]), [rt], [rstd])
            yall = ykall[n_ % 2]
            for k in range(8):
                P.stt(yall.h[:, k, :], b.h[:, k, :], fing.h[:, k:k + 1], rstd.h[:], ALU.mult, ALU.mult, [b, fing, rstd], [yall])
            for t4 in range(4):
                for half in range(2):
                    pq = ps[npp % 4]; npp += 1
                    for kk in range(4):
                        k = half * 4 + kk
                        P.pe(lambda e, o=pq.h[:, kk * 128:(kk + 1) * 128], i=yall.h[:, k, t4 * 128:(t4 + 1) * 128]: e.transpose(o, i, identf.h[:]), [yall, identf], [pq])
                    P.copy("act" if npp % 2 else "dve", otile[t4].h[:, half * 512:(half + 1) * 512], pq.h[:, 0:512], [pq], [otile[t4]])
            for t4 in range(4):
                r0 = c0 - 256 + t4 * 128
                P.dma("sp", out[r0:r0 + 128, :], otile[t4].h[:], [otile[t4]], [])
        P.pop()
    else:
        P.barrier()

    P.emit()
    return nc


def _fm(v, nchunk):
    v = np.asarray(v, np.float32)
    lead = v.shape[:-1]
    r = v.reshape(lead + (nchunk, 128))
    r = np.moveaxis(r, -1, 0)
    return np.ascontiguousarray(r)


def _consts():
    bf = ml_dtypes.bfloat16
    n = np.arange(SEQ, dtype=np.float64)
    ang = 2.0 * np.pi * np.outer(n, n) / SEQ
    dftL = np.stack([np.cos(ang), np.sin(ang)]) / math.sqrt(SEQ)
    m = np.arange(CTX, dtype=np.float64)
    angc = 2.0 * np.pi * np.outer(m, m) / CTX
    dftC = np.concatenate([np.cos(angc), np.sin(angc)], axis=1) / math.sqrt(CTX)
    j = np.arange(128, dtype=np.float64)
    a128 = 2.0 * np.pi * np.outer(j, j) / 128
    d128 = np.concatenate([np.cos(a128), -np.sin(a128)], axis=1) / math.sqrt(128)
    rows = np.repeat(np.arange(SEQ // 64), 64).astype(np.float32)
    cols = np.tile(np.arange(64), SEQ // 64).astype(np.float32)
    inv = (np.float32(10000.0) ** (-np.arange(16, dtype=np.float32) / np.float32(16))).astype(np.float32)
    angr = np.concatenate([rows[:, None] * inv, cols[:, None] * inv], axis=-1).astype(np.float32)
    cos = np.cos(angr).astype(np.float32)
    sin = np.sin(angr).astype(np.float32)
    p = np.arange(128)
    dm = (p % 64) % 32
    rope = np.stack([cos[:, dm].T, sin[:, dm].T], axis=1)
    rotm = np.zeros((128, 128), np.float32)
    for q in range(128):
        d_ = q % 64
        if d_ < 32:
            rotm[q + 32, q] = -1.0
        else:
            rotm[q - 32, q] = 1.0
    return {
        "dftL": dftL.astype(bf), "dftC": dftC.astype(bf), "d128": d128.astype(bf),
        "ropeCS": np.ascontiguousarray(rope, dtype=np.float32).astype(bf), "rotm": rotm.astype(bf),
        "identf": np.eye(128, dtype=np.float32), "identb": np.eye(128, dtype=np.float32).astype(bf),
    }


def make_in_maps(inputs, cores):
    f = lambda a: np.ascontiguousarray(np.asarray(a, np.float32))
    shared = {
        "ada_w": f(inputs["ada_w"]), "w_in": f(inputs["w_in"]),
        "rnn_wr": f(inputs["rnn_wr"]), "rnn_wi": f(inputs["rnn_wi"]),
        "w_branch": f(inputs["w_branch"]), "w_gate": f(inputs["w_gate"]), "w_out": f(inputs["w_out"]),
        "ffn_w1": f(inputs["ffn_w1"]), "ffn_w3": f(inputs["ffn_w3"]), "ffn_w2": f(inputs["ffn_w2"]),
        "attn_lambda": f(inputs["attn_lambda"]).reshape(DEPTH, 256),
        "ada_b_fm": _fm(inputs["ada_b"], 48),
        "conv_w_fm": np.ascontiguousarray(np.transpose(_fm(inputs["rnn_conv_w"], 8), (0, 1, 3, 2))),
        "conv_b_fm": _fm(inputs["rnn_conv_b"], 8),
        "br_fm": _fm(inputs["rnn_br"], 8).reshape(128, 32), "bi_fm": _fm(inputs["rnn_bi"], 8).reshape(128, 32), "lam_fm": _fm(inputs["rnn_lambda"], 8).reshape(128, 32),
        "b_gate_fm": _fm(inputs["b_gate"], 24),
        "final_g_fm": _fm(inputs["final_g"], 8),
    }
    shared.update(_consts())
    x = f(inputs["x"]); ctx = f(inputs["ctx"]); c = f(inputs["c"]); c_ctx = f(inputs["c_ctx"])
    maps = []
    for b in cores:
        m = dict(shared)
        m["x"] = x[b]
        m["ctx"] = ctx[b]
        m["cond"] = np.ascontiguousarray(np.stack([_fm(c[b], 8), _fm(c_ctx, 8)], axis=-1))
        maps.append(m)
    return maps


def kernel(**inputs):
    nc = build_nc()
    in_maps = make_in_maps(inputs, list(range(8)))
    res = run_bass_kernel_spmd(nc, in_maps, core_ids=list(range(8)))
    return np.stack([np.asarray(r["out"], np.float32) for r in res.results], axis=0)
```

```python
import math
import dataclasses
import numpy as np
import ml_dtypes
import concourse.bass as bass
import concourse.mybir as mybir
from concourse.bass_utils import run_bass_kernel_spmd

F32 = mybir.dt.float32
F32R = mybir.dt.float32r
BF16 = mybir.dt.bfloat16
AF = mybir.ActivationFunctionType
ALU = mybir.AluOpType

D = 1024
SEQ = 2048
CTX = 256
T = SEQ + CTX
DEPTH = 2
DFF = 2816
NFF = DFF // 128
EPS = 1e-6
CB = [(0, 256), (256, 512), (768, 512), (1280, 512), (1792, 512)]
GW = 2307
GB = [(0, 512), (512, 512), (1024, 512), (1536, 512), (2048, 259)]
ARENA0 = 16896
DTSIZE = {F32: 4, F32R: 4, BF16: 2}
ARENA1 = 229376


class Buf:
    __slots__ = ("lw", "rd")

    def __init__(self):
        self.lw = None
        self.rd = []


class Tile:
    def __init__(self, h, psum=False):
        self.h = h
        self.buf = Buf()
        self.psum = psum

    def __getitem__(self, k):
        return self.h[k]


class Op:
    __slots__ = ("id", "stream", "is_dma", "fn", "waits", "signal", "sigval", "dsem", "dval", "scope")


class Prog:
    COMPUTE = ("pe", "act", "dve", "pool")

    def __init__(self, nc, dma_depth=12):
        self.nc = nc
        self.ops = []
        self.eng = {"pe": nc.tensor, "act": nc.scalar, "dve": nc.vector, "pool": nc.gpsimd, "sp": nc.sync}
        self.tl_sem = {s: nc.alloc_semaphore("tl_" + s) for s in self.COMPUTE}
        self.Dq = dma_depth
        self.dma_sems = {q: [nc.alloc_semaphore(f"dq_{q}_{i}") for i in range(dma_depth)] for q in ("sp", "pool")}
        self.dma_ops = {"sp": [], "pool": []}
        self.last_op = {s: None for s in self.COMPUTE}
        self.waited_tl = {s: {p: -1 for p in self.COMPUTE} for s in list(self.COMPUTE) + ["sp"]}
        self.waited_dma = {s: {} for s in list(self.COMPUTE) + ["sp"]}
        self.barrier_id = 0
        self.sbuf_off = ARENA0
        self.sbuf_mark = []
        self.ntile = 0
        self.scope = None
        self.use_scopes = False

    def sb(self, shape, dtype, name=None):
        nbytes = int(np.prod(shape[1:])) * DTSIZE[dtype]
        nbytes = (nbytes + 31) // 32 * 32
        off = self.sbuf_off
        assert off + nbytes <= ARENA1, f"SBUF overflow {off}+{nbytes}"
        self.sbuf_off += nbytes
        self.ntile += 1
        h = self.nc.alloc_sbuf_tensor_at(f"{name or 't'}_{self.ntile}", list(shape), dtype, offset=off)
        return Tile(h)

    def push(self):
        self.sbuf_mark.append(self.sbuf_off)

    def pop(self):
        self.barrier()
        self.sbuf_off = self.sbuf_mark.pop()

    def _add(self, stream, is_dma, fn, reads, writes, strict=()):
        op = Op()
        op.id = len(self.ops)
        op.stream = stream
        op.is_dma = is_dma
        op.fn = fn
        op.signal = False
        op.sigval = None
        op.waits = []
        op.dsem = None
        op.scope = self.scope
        deps = set()
        for t in reads:
            b = t.buf
            if b.lw is not None:
                deps.add(b.lw)
            if t.psum:
                deps.update(b.rd)
        for t in writes:
            b = t.buf
            if b.lw is not None:
                deps.add(b.lw)
            deps.update(b.rd)
        for d in sorted(deps):
            if d < self.barrier_id:
                continue
            dop = self.ops[d]
            self._wait_on(op, dop)
        if strict is True:
            strict = list(reads) + list(writes)
        for t in strict:
            d = t.buf.lw
            if d is not None and d >= self.barrier_id and self.ops[d].stream == stream and not self.ops[d].is_dma:
                dop = self.ops[d]
                if self.waited_tl[stream][stream] < dop.id:
                    self.waited_tl[stream][stream] = dop.id
                    dop.signal = True
                    op.waits.append(("tl", dop))
        if is_dma:
            q = stream
            lst = self.dma_ops[q]
            i = len(lst)
            op.dsem = self.dma_sems[q][i % self.Dq]
            op.dval = 16 * (i // self.Dq + 1)
            if i >= self.Dq:
                self._wait_on(op, lst[i - self.Dq])
            lst.append(op)
        else:
            self.last_op[stream] = op
        for t in reads:
            t.buf.rd.append(op.id)
        for t in writes:
            t.buf.lw = op.id
            t.buf.rd = []
        self.ops.append(op)
        return op

    def _wait_on(self, op, dop):
        s = op.stream
        if dop.is_dma:
            key = id(dop.dsem)
            if self.waited_dma[s].get(key, 0) >= dop.dval:
                return
            self.waited_dma[s][key] = dop.dval
            op.waits.append(("dma", dop))
        else:
            if dop.stream == s:
                return
            if self.waited_tl[s][dop.stream] >= dop.id:
                return
            self.waited_tl[s][dop.stream] = dop.id
            dop.signal = True
            op.waits.append(("tl", dop))

    def barrier(self):
        streams = list(self.COMPUTE) + ["sp"]
        for s in streams:
            op = Op()
            op.id = len(self.ops)
            op.stream = s
            op.is_dma = False
            op.fn = None
            op.signal = False
            op.sigval = None
            op.waits = []
            op.dsem = None
            op.scope = self.scope
            for p in self.COMPUTE:
                lo = self.last_op[p]
                if lo is not None and lo.id >= 0:
                    self._wait_on(op, lo)
            for q in ("sp", "pool"):
                for dop in self.dma_ops[q][-self.Dq:]:
                    self._wait_on(op, dop)
            self.ops.append(op)
        self.barrier_id = len(self.ops)

    def pe(self, fn, reads, writes):
        return self._add("pe", False, fn, reads, writes)

    def act(self, fn, reads, writes, strict=()):
        return self._add("act", False, fn, reads, writes, strict)

    def dve(self, fn, reads, writes, strict=()):
        return self._add("dve", False, fn, reads, writes, strict)

    def pool(self, fn, reads, writes):
        return self._add("pool", False, fn, reads, writes)

    def dma(self, q, out, in_, reads, writes):
        return self._add(q, True, lambda e: e.dma_start(out=out, in_=in_), reads, writes)

    def mm(self, out, lhsT, rhs, start, stop, reads, writes, ldw=True):
        if ldw:
            return self.pe(lambda e: e.matmul(out, lhsT=lhsT, rhs=rhs, start=start, stop=stop), reads, writes)
        nc = self.nc

        def fn(e):
            with nc.discard():
                bi = e.matmul(out, lhsT=lhsT, rhs=rhs, start=start, stop=stop)
            return e.add_instruction(dataclasses.replace(bi.ins, ldweights=False, name=nc.get_next_instruction_name()))
        return self.pe(fn, reads, writes)

    def actf(self, out, in_, func, reads, writes, bias=None, scale=None, accum_out=None, strict=()):
        kw = {}
        if bias is not None:
            kw["bias"] = bias
        if scale is not None:
            kw["scale"] = scale
        if accum_out is not None:
            kw["accum_out"] = accum_out
        return self.act(lambda e: e.activation(out=out, in_=in_, func=func, **kw), reads, writes, strict)

    def tt(self, stream, out, in0, in1, op, reads, writes, strict=()):
        return self._add(stream, False, lambda e: e.tensor_tensor(out=out, in0=in0, in1=in1, op=op), reads, writes, strict)

    def ts(self, stream, out, in0, s1, s2, op0, op1, reads, writes, strict=()):
        if s2 is None:
            return self._add(stream, False, lambda e: e.tensor_scalar(out=out, in0=in0, scalar1=s1, scalar2=None, op0=op0), reads, writes, strict)
        return self._add(stream, False, lambda e: e.tensor_scalar(out=out, in0=in0, scalar1=s1, scalar2=s2, op0=op0, op1=op1), reads, writes, strict)

    def stt(self, out, in0, scalar, in1, op0, op1, reads, writes, strict=()):
        return self.dve(lambda e: e.scalar_tensor_tensor(out=out, in0=in0, scalar=scalar, in1=in1, op0=op0, op1=op1), reads, writes, strict)

    def copy(self, stream, out, in_, reads, writes):
        if stream == "act":
            return self.act(lambda e: e.copy(out=out, in_=in_), reads, writes)
        return self._add(stream, False, lambda e: e.tensor_copy(out=out, in_=in_), reads, writes)

    def emit(self):
        cnt = {s: 0 for s in self.COMPUTE}
        for op in self.ops:
            if op.signal:
                cnt[op.stream] += 1
                op.sigval = cnt[op.stream]
        cur = None
        ctx = None
        for op in self.ops:
            if self.use_scopes and op.scope != cur:
                if ctx is not None:
                    ctx.__exit__(None, None, None)
                    ctx = None
                cur = op.scope
                if cur is not None:
                    ctx = self.nc.named_scope(cur)
                    ctx.__enter__()
            e = self.eng[op.stream]
            for kind, dop in op.waits:
                if kind == "dma":
                    e.wait_ge(dop.dsem, dop.dval)
                else:
                    e.wait_ge(self.tl_sem[dop.stream], dop.sigval)
            if op.fn is None:
                continue
            ins = op.fn(e)
            if op.is_dma:
                ins.then_inc(op.dsem, 16)
            elif op.signal:
                ins.then_inc(self.tl_sem[op.stream], 1)
        if ctx is not None:
            ctx.__exit__(None, None, None)


def rev(ap):
    apl = [list(p) for p in ap.ap]
    step, cnt = apl[-1]
    off = ap.offset + step * (cnt - 1)
    apl[-1] = [-step, cnt]
    return bass.AP(ap.tensor, off, apl)


def build_nc(debug=None, stop_after=None, scopes=False):
    debug = debug or set()
    nc = bass.Bass("TRN2", target_bir_lowering=False)

    def din(name, shape, dt=F32):
        return nc.dram_tensor(name, list(shape), dt, kind="ExternalInput").ap()

    def dscr(name, shape, dt):
        kind = "ExternalOutput" if name in debug else "Internal"
        return nc.dram_tensor(name, list(shape), dt, kind=kind).ap()

    x_in = din("x", [SEQ, D])
    ctx_in = din("ctx", [CTX, D])
    cond_in = din("cond", [128, 8, 2])
    ada_w = din("ada_w", [DEPTH, D, 6 * D])
    ada_b_in = din("ada_b_fm", [128, DEPTH, 48])
    w_in = din("w_in", [DEPTH, D, 6 * D])
    convw_in = din("conv_w_fm", [128, DEPTH, 8, 4])
    convb_in = din("conv_b_fm", [128, DEPTH, 8])
    rnn_wr = din("rnn_wr", [DEPTH, 2, 8, 128, 128])
    rnn_wi = din("rnn_wi", [DEPTH, 2, 8, 128, 128])
    br_in = din("br_fm", [128, 32])
    bi_in = din("bi_fm", [128, 32])
    lam_in = din("lam_fm", [128, 32])
    attn_lam_in = din("attn_lambda", [DEPTH, 256])
    w_branch = din("w_branch", [DEPTH, 3 * D, D])
    w_gate = din("w_gate", [DEPTH, D, 3 * D])
    bgate_in = din("b_gate_fm", [128, DEPTH, 24])
    w_out = din("w_out", [DEPTH, D, D])
    ffn_w1 = din("ffn_w1", [DEPTH, D, DFF])
    ffn_w3 = din("ffn_w3", [DEPTH, D, DFF])
    ffn_w2 = din("ffn_w2", [DEPTH, DFF, D])
    fing_in = din("final_g_fm", [128, 8])
    dftL_in = din("dftL", [2, SEQ, SEQ], BF16)
    dftC_in = din("dftC", [CTX, 512], BF16)
    d128_in = din("d128", [128, 256], BF16)
    d128p_in = din("d128p", [128, 256], BF16)
    sgn_in = din("sgn", [128, 2], BF16)
    rope_in = din("ropeCS", [128, 2, SEQ], BF16)
    rotm_in = din("rotm", [128, 128], BF16)
    identf_in = din("identf", [128, 128])
    identb_in = din("identb", [128, 128], BF16)

    out = nc.dram_tensor("out", [SEQ, D], F32, kind="ExternalOutput").ap()
    xS = dscr("xS", [128, 8, T], F32)
    yaS = dscr("yaS", [128, 8, T], BF16)
    ybS = dscr("ybS", [128, 8, T], BF16)
    ycS = dscr("ycS", [128, 8, T], BF16)
    mixS = dscr("mixS", [128, 8, T], BF16)
    dbg_h = dscr("dbg_h", [128, 8, T], BF16) if "dbg_h" in debug else None
    dbg_mods = dscr("dbg_mods", [128, DEPTH, 48, 2], F32) if "dbg_mods" in debug else None

    P = Prog(nc)
    P.use_scopes = scopes
    P.scope = "setup"
    ps = [Tile(nc.alloc_psum_tensor(f"psum{i}", [128, 512], F32), psum=True) for i in range(8)]
    bank_rr = [0]

    def take(n):
        r = [ps[(bank_rr[0] + i) % 8] for i in range(n)]
        bank_rr[0] += n
        return r

    def group_mm(banks, cis, nk, lhs_fn, rhs_fn, reads_w, reads_x):
        for k in range(nk):
            for j, ci in enumerate(cis):
                w = CB[ci][1]
                P.mm(banks[j].h[:, 0:w], lhs_fn(k), rhs_fn(k, ci), k == 0, k == nk - 1, [reads_w, reads_x], [banks[j]], ldw=(j == 0))

    identf = P.sb([128, 128], F32, "identf")
    identb = P.sb([128, 128], BF16, "identb")
    ones_bf = P.sb([128, 128], BF16, "ones")
    rotm = P.sb([128, 128], BF16, "rotm")
    d128 = P.sb([128, 256], BF16, "d128")
    d128p = P.sb([128, 256], BF16, "d128p")
    sgn = P.sb([128, 2], BF16, "sgn")
    dC = P.sb([128, 2, 512], BF16, "dC")
    mods = P.sb([128, DEPTH, 48, 2], F32, "mods")
    convw = P.sb([128, DEPTH, 8, 4], F32, "convw")
    convb = P.sb([128, DEPTH, 8], F32, "convb")
    br_t = P.sb([128, 32], F32, "br")
    bi_t = P.sb([128, 32], F32, "bi")
    lam_t = P.sb([128, 32], F32, "lam")
    c1_t = P.sb([128, 32], F32, "c1")
    c2_t = P.sb([128, 32], F32, "c2")
    bgate = P.sb([128, DEPTH, 24], F32, "bgate")
    fing = P.sb([128, 8], F32, "fing")
    eps_t = P.sb([128, 1], F32, "eps")
    one_t = P.sb([128, 1], F32, "one")
    nlam = P.sb([128, DEPTH], F32, "nlam")
    hT = P.sb([128, 8, T], BF16, "hT")

    for dst, src in [(identf, identf_in), (identb, identb_in), (rotm, rotm_in), (d128, d128_in), (d128p, d128p_in), (sgn, sgn_in),
                     (convw, convw_in), (convb, convb_in), (br_t, br_in), (bi_t, bi_in), (lam_t, lam_in),
                     (bgate, bgate_in), (fing, fing_in)]:
        P.dma("sp", dst.h[:], src, [], [dst])
    P.dma("sp", dC.h[:], dftC_in.rearrange("(a p) k -> p a k", p=128), [], [dC])
    P.pool(lambda e: e.memset(ones_bf.h[:], 1.0), [], [ones_bf])
    P.pool(lambda e: e.memset(eps_t.h[:], EPS), [], [eps_t])
    P.pool(lambda e: e.memset(one_t.h[:], 1.0), [], [one_t])

    P.actf(c1_t.h[:], lam_t.h[:], AF.Exp, [lam_t], [c1_t], scale=-1.0)
    P.actf(c1_t.h[:], c1_t.h[:], AF.Ln, [c1_t, one_t], [c1_t], bias=one_t.h[:], strict=True)
    P.ts("dve", c2_t.h[:], c1_t.h[:], -16.0, None, ALU.mult, None, [c1_t], [c2_t])
    P.ts("dve", c1_t.h[:], c1_t.h[:], -8.0, None, ALU.mult, None, [c1_t], [c1_t], strict=True)

    P.push()
    al = P.sb([1, DEPTH, 256], F32, "al")
    al2 = P.sb([1, DEPTH, 2, 64], F32, "al2")
    al3 = P.sb([1, DEPTH, 2], F32, "al3")
    al4 = P.sb([1, DEPTH], F32, "al4")
    onesf = P.sb([1, 128], F32, "onesf")
    P.pool(lambda e: e.memset(onesf.h[:], 1.0), [], [onesf])
    P.dma("sp", al.h[:], attn_lam_in.rearrange("(o l) f -> o l f", o=1), [], [al])
    for l in range(DEPTH):
        lam_init = 0.8 - 0.6 * math.exp(-0.3 * l)
        for j in range(2):
            a0 = al.h[:, l, (2 * j) * 64:(2 * j + 1) * 64]
            a1 = al.h[:, l, (2 * j + 1) * 64:(2 * j + 2) * 64]
            P.tt("dve", al2.h[:, l, j, :], a0, a1, ALU.mult, [al], [al2], strict=True)
            P.dve(lambda e, o=al3.h[:, l, j:j + 1], i=al2.h[:, l, j, :]: e.reduce_sum(out=o, in_=i, axis=mybir.AxisListType.X), [al2], [al3], strict=True)
        P.actf(al3.h[:, l, :], al3.h[:, l, :], AF.Exp, [al3], [al3], strict=True)
        P.tt("dve", al4.h[:, l:l + 1], al3.h[:, l, 1:2], al3.h[:, l, 0:1], ALU.subtract, [al3], [al4], strict=True)
        P.ts("dve", al4.h[:, l:l + 1], al4.h[:, l:l + 1], -lam_init, None, ALU.add, None, [al4], [al4], strict=True)
    P.mm(ps[0].h[:, 0:DEPTH], onesf.h[:], al4.h[:], True, True, [onesf, al4], [ps[0]])
    P.copy("dve", nlam.h[:], ps[0].h[:, 0:DEPTH], [ps[0]], [nlam])
    P.pop()

    P.scope = "adaln"
    P.push()
    cond = P.sb([128, 8, 2], F32, "cond")
    condr = P.sb([128, 8, 2], F32R, "condr")
    adab = P.sb([128, DEPTH, 48], F32, "adab")
    wA = [P.sb([128, 8, 512], F32R, "wA") for _ in range(2)]
    P.dma("sp", cond.h[:], cond_in, [], [cond])
    P.dma("sp", adab.h[:], ada_b_in, [], [adab])
    P.actf(condr.h[:], cond.h[:], AF.Silu, [cond], [condr])
    xin = [P.sb([128, D], F32, "xin") for _ in range(2)]
    xst = [P.sb([128, 8, 512], F32, "xst") for _ in range(2)]

    def gen_stage0():
        nt = 0
        for ci, (c0, w) in enumerate(CB):
            st = xst[ci % 2]
            for t4 in range(w // 128):
                tok0 = c0 + t4 * 128
                xi = xin[nt % 2]
                src = ctx_in[tok0:tok0 + 128, :] if tok0 < CTX else x_in[tok0 - CTX:tok0 - CTX + 128, :]
                P.dma("sp", xi.h[:], src, [], [xi])
                for half in range(2):
                    pt = ps[4 + (nt * 2 + half) % 4]
                    for kk in range(4):
                        k = half * 4 + kk
                        P.pe(lambda e, o=pt.h[:, kk * 128:(kk + 1) * 128], i=xi.h[:, k * 128:(k + 1) * 128]: e.transpose(o, i, identf.h[:]), [xi, identf], [pt])
                    P.copy("act" if half == 0 else "dve", st.h[:, half * 4:(half + 1) * 4, t4 * 128:(t4 + 1) * 128],
                           pt.h[:, 0:512].rearrange("p (a b) -> p a b", b=128), [pt], [st])
                nt += 1
                if t4 == w // 128 - 1:
                    P.dma("sp", xS[:, :, c0:c0 + w], st.h[:, :, 0:w], [st], [])
                yield

    g0 = gen_stage0()
    nblk = 0
    for l in range(DEPTH):
        pm = ps[l]
        for blk in range(12):
            w = wA[nblk % 2]
            nblk += 1
            P.dma("pool", w.h[:], ada_w[l][:, blk * 512:(blk + 1) * 512].rearrange("(k p) f -> p k f", p=128), [], [w])
            for j in range(4):
                oc = blk * 4 + j
                for k in range(8):
                    P.mm(pm.h[:, oc * 2:oc * 2 + 2], w.h[:, k, j * 128:(j + 1) * 128], condr.h[:, k, :], k == 0, k == 7, [w, condr], [pm])
            next(g0, None)
        pv = pm.h[:, 0:96].rearrange("p (a b) -> p a b", b=2)
        for j in range(2):
            P.tt("dve", mods.h[:, l, :, j], pv[:, :, j], adab.h[:, l, :], ALU.add, [pm, adab], [mods], strict=True)
        for m in (1, 4):
            P.ts("dve", mods.h[:, l, m * 8:(m + 1) * 8, :], mods.h[:, l, m * 8:(m + 1) * 8, :], 1.0, None, ALU.add, None, [mods], [mods], strict=True)
    if dbg_mods is not None:
        P.dma("sp", dbg_mods, mods.h[:], [mods], [])
    for _ in g0:
        pass
    P.pop()

    def mod(l, m, k, j):
        return mods.h[:, l, m * 8 + k, j:j + 1]

    def norm_block(l, sub, ci, xb, sq, rt, rstd, tmp):
        c0, w = CB[ci]
        j = 1 if ci == 0 else 0
        msh, msc = (0, 1) if sub == 1 else (3, 4)
        pss = ps[4 + (ci % 2)]
        P.actf(sq.h[:, :, 0:w], xb.h[:, :, 0:w], AF.Square, [xb], [sq])
        for k in range(8):
            P.mm(pss.h[:, 0:w], ones_bf.h[:], sq.h[:, k, 0:w], k == 0, k == 7, [ones_bf, sq], [pss])
        P.actf(rt.h[:, 0:w], pss.h[:, 0:w], AF.Sqrt, [pss, eps_t], [rt], bias=eps_t.h[:], scale=1.0 / D)
        P.dve(lambda e: e.reciprocal(out=rstd.h[:, 0:w], in_=rt.h[:, 0:w]), [rt], [rstd])
        for k in range(8):
            tm = tmp[k % 2]
            P.tt("dve", tm.h[:, 0:w], xb.h[:, k, 0:w], rstd.h[:, 0:w], ALU.mult, [xb, rstd], [tm])
            P.actf(hT.h[:, k, c0:c0 + w], tm.h[:, 0:w], AF.Identity, [tm, mods], [hT], bias=mod(l, msh, k, j), scale=mod(l, msc, k, j))

    def norm_tiles():
        sq = P.sb([128, 8, 512], BF16, "sq")
        rt = P.sb([128, 512], F32, "rt")
        rstd = P.sb([128, 512], F32, "rstd")
        tmp = [P.sb([128, 512], F32, "ntmp") for _ in range(2)]
        return sq, rt, rstd, tmp

    def proj_fm(wt, wsl, ci, pst):
        c0, w = CB[ci]
        for k in range(8):
            P.mm(pst.h[:, 0:w], wt.h[:, k, wsl], hT.h[:, k, c0:c0 + w], k == 0, k == 7, [wt, hT], [pst])

    nlayers = DEPTH
    for l in range(nlayers):
        last = l == DEPTH - 1
        lam_init = 0.8 - 0.6 * math.exp(-0.3 * l)
        cbs = list(range(1, 5)) if last else list(range(5))

        P.scope = f"L{l}_n1"
        P.push()
        xb = [P.sb([128, 8, 512], F32, "xb") for _ in range(2)]
        nt_ = norm_tiles()
        for ci, (c0, w) in enumerate(CB):
            b = xb[ci % 2]
            P.dma("sp", b.h[:, :, 0:w], xS[:, :, c0:c0 + w], [], [b])
            norm_block(l, 1, ci, b, *nt_)
        if dbg_h is not None and l == 0:
            P.dma("sp", dbg_h, hT.h[:], [hT], [])
        P.pop()
        if stop_after == "n1":
            break

        P.scope = f"L{l}_rnn"
        P.push()
        xrp2 = [P.sb([128, GW + 3], F32, "xrp") for _ in range(2)]
        u2 = [P.sb([128, GW], F32, "u") for _ in range(2)]
        A2 = [P.sb([128, GW], F32, "A") for _ in range(2)]
        M2 = [P.sb([128, GW], F32, "M") for _ in range(2)]
        I2 = [P.sb([128, GW], F32, "I") for _ in range(2)]
        H = [P.sb([128, GW], F32, "H") for _ in range(2)]
        ub2 = [P.sb([128, GW], BF16, "ub") for _ in range(2)]
        gg2 = [P.sb([128, GW], BF16, "gg") for _ in range(2)]
        yat = [P.sb([128, GW], BF16, "yat") for _ in range(2)]
        wxg = [P.sb([128, 8, 256], BF16, "wxg") for _ in range(2)]
        wri = P.sb([128, 4, 8, 128], BF16, "wri")
        for t_ in xrp2 + gg2:
            P.pool(lambda e, t_=t_: e.memset(t_.h[:], 0.0), [], [t_])
        for t_ in H:
            P.pool(lambda e, t_=t_: e.memset(t_.h[:], 0.0), [], [t_])
        for d in range(2):
            P.dma("pool", wri.h[:, d * 2 + 0, :, :], rnn_wr[l, d].rearrange("g i j -> i g j"), [], [wri])
            P.dma("pool", wri.h[:, d * 2 + 1, :, :], rnn_wi[l, d].rearrange("g i j -> i g j"), [], [wri])
        npp = 0
        for g in range(8):
            xrp, u, ub, gg = xrp2[g % 2], u2[g % 2], ub2[g % 2], gg2[g % 2]
            wt = wxg[g % 2]
            P.dma("pool", wt.h[:, :, 0:128], w_in[l][:, g * 128:(g + 1) * 128].rearrange("(k p) f -> p k f", p=128), [], [wt])
            P.dma("pool", wt.h[:, :, 128:256], w_in[l][:, D + g * 128:D + (g + 1) * 128].rearrange("(k p) f -> p k f", p=128), [], [wt])
            for ci, (c0, w) in enumerate(CB):
                go = 0 if ci == 0 else 259 + (c0 - 256)
                pst = ps[npp % 4]; npp += 1
                proj_fm(wt, slice(0, 128), ci, pst)
                P.copy("act", xrp.h[:, go + 2:go + 2 + w], pst.h[:, 0:w], [pst], [xrp])
            for ci, (c0, w) in enumerate(CB):
                go = 0 if ci == 0 else 259 + (c0 - 256)
                pst = ps[npp % 4]; npp += 1
                proj_fm(wt, slice(128, 256), ci, pst)
                P.actf(gg.h[:, go:go + w], pst.h[:, 0:w], AF.Gelu, [pst], [gg])
            P.ts("dve", u.h[:], xrp.h[:, 0:GW], convw.h[:, l, g, 0:1], convb.h[:, l, g:g + 1], ALU.mult, ALU.add, [xrp, convw, convb], [u])
            for k in range(1, 4):
                P.stt(u.h[:], xrp.h[:, k:k + GW], convw.h[:, l, g, k:k + 1], u.h[:], ALU.mult, ALU.add, [xrp, u, convw], [u])
            P.copy("pool", ub.h[:], u.h[:], [u], [ub])
            for d in range(2):
                A, I = A2[d], I2[d]
                ix = (l * 2 + d) * 8 + g
                for (gc0, gw_) in GB:
                    pst = ps[npp % 4]; npp += 1
                    P.mm(pst.h[:, 0:gw_], wri.h[:, d * 2, g, :], ub.h[:, gc0:gc0 + gw_], True, True, [wri, ub], [pst])
                    P.actf(A.h[:, gc0:gc0 + gw_], pst.h[:, 0:gw_], AF.Sigmoid, [pst, br_t], [A], bias=br_t.h[:, ix:ix + 1])
                for (gc0, gw_) in GB:
                    pst = ps[npp % 4]; npp += 1
                    P.mm(pst.h[:, 0:gw_], wri.h[:, d * 2 + 1, g, :], ub.h[:, gc0:gc0 + gw_], True, True, [wri, ub], [pst])
                    P.actf(I.h[:, gc0:gc0 + gw_], pst.h[:, 0:gw_], AF.Sigmoid, [pst, bi_t], [I], bias=bi_t.h[:, ix:ix + 1])
                P.tt("pool", I.h[:], I.h[:], u.h[:], ALU.mult, [I, u], [I])
            for d in range(2):
                A, M = A2[d], M2[d]
                ix = (l * 2 + d) * 8 + g
                P.actf(M.h[:], A.h[:], AF.Exp, [A, c2_t], [M], scale=c2_t.h[:, ix:ix + 1])
                P.actf(A.h[:], A.h[:], AF.Exp, [A, c1_t], [A], scale=c1_t.h[:, ix:ix + 1])
            for d in range(2):
                M = M2[d]
                P.actf(M.h[:], M.h[:], AF.Sqrt, [M, one_t], [M], bias=one_t.h[:], scale=-1.0)
            for d in range(2):
                A, M, I = A2[d], M2[d], I2[d]
                P.tt("dve", I.h[:], I.h[:], M.h[:], ALU.mult, [I, M], [I])
                Hd = H[d]
                if d == 0:
                    P.dve(lambda e, o=Hd.h[:, 0:256], a=A.h[:, 0:256], b=I.h[:, 0:256]: e.tensor_tensor_scan(out=o, data0=a, data1=b, initial=0.0, op0=ALU.mult, op1=ALU.add), [A, I], [Hd])
                    P.dve(lambda e, o=Hd.h[:, 259:GW], a=A.h[:, 259:GW], b=I.h[:, 259:GW], i0=Hd.h[:, 255:256]: e.tensor_tensor_scan(out=o, data0=a, data1=b, initial=i0, op0=ALU.mult, op1=ALU.add), [A, I, Hd], [Hd], strict=[Hd])
                else:
                    P.dve(lambda e, o=rev(Hd.h[:, 0:256]), a=rev(A.h[:, 0:256]), b=rev(I.h[:, 0:256]): e.tensor_tensor_scan(out=o, data0=a, data1=b, initial=0.0, op0=ALU.mult, op1=ALU.add), [A, I], [Hd])
                    P.dve(lambda e, o=rev(Hd.h[:, 259:GW]), a=rev(A.h[:, 259:GW]), b=rev(I.h[:, 259:GW]), i0=Hd.h[:, 0:1]: e.tensor_tensor_scan(out=o, data0=a, data1=b, initial=i0, op0=ALU.mult, op1=ALU.add), [A, I, Hd], [Hd], strict=[Hd])
            yt = yat[g % 2]
            P.tt("dve", H[0].h[:], H[0].h[:], H[1].h[:], ALU.add, [H[0], H[1]], [H[0]])
            P.tt("dve", yt.h[:], H[0].h[:], gg.h[:], ALU.mult, [H[0], gg], [yt])
            if not last:
                P.dma("sp", yaS[:, g, 0:256], yt.h[:, 0:256], [yt], [])
            P.dma("sp", yaS[:, g, 256:T], yt.h[:, 259:GW], [yt], [])
        P.pop()
        if stop_after == "rnn":
            break

        P.scope = f"L{l}_fourier"
        P.push()
        xf = P.sb([128, 18, D], BF16, "xf")
        wxf = [P.sb([128, 8, 512], BF16, "wxf") for _ in range(2)]
        Dt = [P.sb([128, 16, 512], BF16, "Dt") for _ in range(2)]
        Pcs = P.sb([128, 2, 8, 1026], BF16, "Pcs")
        ybb = [P.sb([128, 8, 512], BF16, "ybb") for _ in range(2)]
        npp = 0
        tts = list(range(2, 18)) if last else list(range(18))
        for half in range(2):
            wt = wxf[half]
            P.dma("pool", wt.h[:], w_in[l][:, 2 * D + half * 512:2 * D + (half + 1) * 512].rearrange("(k p) f -> p k f", p=128), [], [wt])
            for tt_ in tts:
                pst = ps[npp % 4]; npp += 1
                for k in range(8):
                    P.mm(pst.h[:, 0:512], hT.h[:, k, tt_ * 128:(tt_ + 1) * 128], wt.h[:, k, :], k == 0, k == 7, [hT, wt], [pst])
                P.copy("act" if npp % 2 else "dve", xf.h[:, tt_, half * 512:(half + 1) * 512], pst.h[:, 0:512], [pst], [xf])
        nd = 0
        P.pool(lambda e: e.memset(Pcs.h[:, 1, :, 1024:1026], 0.0), [], [Pcs])
        for kb in range(2):
            for cs in range(2):
                dt_ = Dt[nd % 2]; nd += 1
                P.dma("sp", dt_.h[:], dftL_in[cs][:, kb * 512:(kb + 1) * 512].rearrange("(a p) k -> p a k", p=128), [], [dt_])
                for g in range(8):
                    pst = ps[npp % 4]; npp += 1
                    for a in range(16):
                        P.mm(pst.h[:, 0:512], xf.h[:, 2 + a, g * 128:(g + 1) * 128], dt_.h[:, a, :], a == 0, a == 15, [xf, dt_], [pst])
                    P.copy("act" if g % 2 else "dve", Pcs.h[:, cs, g, kb * 512:(kb + 1) * 512], pst.h[:, 0:512], [pst], [Pcs])
        for g in range(8):
            pst = ps[npp % 4]; npp += 1
            for a in range(16):
                P.mm(pst.h[:, 0:2], xf.h[:, 2 + a, g * 128:(g + 1) * 128], sgn.h[:, 0:2], a == 0, a == 15, [xf, sgn], [pst])
            P.copy("dve", Pcs.h[:, 0, g, 1024:1026], pst.h[:, 0:2], [pst], [Pcs])
        nyb = 0
        for kb in range(2):
            yb_ = ybb[nyb % 2]; nyb += 1
            for g in range(8):
                pst = ps[4 + npp % 2]; npp += 1
                P.mm(pst.h[:, 0:512], d128.h[:, 0:128], Pcs.h[:, 0, g, kb * 512:(kb + 1) * 512], True, False, [d128, Pcs], [pst])
                P.mm(pst.h[:, 0:512], d128.h[:, 128:256], Pcs.h[:, 1, g, kb * 512:(kb + 1) * 512], False, True, [d128, Pcs], [pst])
                P.copy("act" if g % 2 else "dve", yb_.h[:, g, :], pst.h[:, 0:512], [pst], [yb_])
            P.dma("sp", ybS[:, :, 256 + kb * 512:256 + (kb + 1) * 512], yb_.h[:], [yb_], [])
        for kb in range(2):
            yb_ = ybb[nyb % 2]; nyb += 1
            k0 = kb * 512 + 1
            for g in range(8):
                pst = ps[4 + npp % 2]; npp += 1
                P.mm(pst.h[:, 0:512], d128p.h[:, 0:128], Pcs.h[:, 0, g, k0:k0 + 512], True, False, [d128p, Pcs], [pst])
                P.mm(pst.h[:, 0:512], d128p.h[:, 128:256], Pcs.h[:, 1, g, k0:k0 + 512], False, True, [d128p, Pcs], [pst])
                P.copy("dve", rev(yb_.h[:, g, :]), pst.h[:, 0:512], [pst], [yb_])
            o0 = SEQ - kb * 512 - 512
            P.dma("sp", ybS[:, :, 256 + o0:256 + o0 + 512], yb_.h[:], [yb_], [])
        if not last:
            ybc = P.sb([128, 8, 256], BF16, "ybc")
            Pc = [P.sb([128, 512], BF16, "Pc") for _ in range(2)]
            for g in range(8):
                pst = ps[npp % 4]; npp += 1
                pc_ = Pc[g % 2]
                for a in range(2):
                    P.mm(pst.h[:, 0:512], xf.h[:, a, g * 128:(g + 1) * 128], dC.h[:, a, :], a == 0, a == 1, [xf, dC], [pst])
                P.copy("act", pc_.h[:], pst.h[:, 0:512], [pst], [pc_])
                pst2 = ps[4 + npp % 2]; npp += 1
                P.mm(pst2.h[:, 0:256], d128.h[:, 0:128], pc_.h[:, 0:256], True, False, [d128, pc_], [pst2])
                P.mm(pst2.h[:, 0:256], d128.h[:, 128:256], pc_.h[:, 256:512], False, True, [d128, pc_], [pst2])
                P.copy("dve", ybc.h[:, g, :], pst2.h[:, 0:256], [pst2], [ybc])
            P.dma("sp", ybS[:, :, 0:256], ybc.h[:], [ybc], [])
        P.pop()
        if stop_after == "fourier":
            break

        P.scope = f"L{l}_attn"
        P.push()
        vt = P.sb([128, 18, 8, 130], BF16, "vt")
        rope = P.sb([128, 2, SEQ], BF16, "rope")
        tb = [P.sb([128, 512], BF16, "tb") for _ in range(2)]
        tc_ = [P.sb([128, 512], F32, "tc") for _ in range(2)]
        ts_ = [P.sb([128, 512], F32, "ts") for _ in range(2)]
        eT = [P.sb([128, 18, 512], BF16, "eT") for _ in range(2)]
        obt = [P.sb([128, 4, 2, 130], F32, "obt") for _ in range(2)]
        s16 = [P.sb([128, 16], F32, "s16") for _ in range(2)]
        o2t = [P.sb([128, 4, 128], F32, "o2t") for _ in range(2)]
        t1t = [P.sb([128, 4, 128], F32, "t1t") for _ in range(2)]
        yc4 = [P.sb([128, 4, 128], BF16, "yc4") for _ in range(2)]
        kk_ = 1.0 - lam_init
        epsk = P.sb([128, 1], F32, "epsk")
        P.pool(lambda e: e.memset(epsk.h[:], EPS / (kk_ * kk_)), [], [epsk])
        ycT = [P.sb([128, T], BF16, "ycT") for _ in range(2)]
        P.dma("sp", rope.h[:], rope_in, [], [rope])
        P.pool(lambda e: e.memset(vt.h[:], 1.0), [], [vt])
        npp = 0
        P.push()
        wv = [P.sb([128, 8, 512], BF16, "wv") for _ in range(2)]
        for half in range(2):
            wt = wv[half]
            P.dma("pool", wt.h[:], w_in[l][:, 5 * D + half * 512:5 * D + (half + 1) * 512].rearrange("(k p) f -> p k f", p=128), [], [wt])
            for tt_ in range(18):
                pst = ps[npp % 4]; npp += 1
                for k in range(8):
                    P.mm(pst.h[:, 0:512], hT.h[:, k, tt_ * 128:(tt_ + 1) * 128], wt.h[:, k, :], k == 0, k == 7, [hT, wt], [pst])
                P.copy("act" if npp % 2 else "dve", vt.h[:, tt_, half * 4:(half + 1) * 4, 0:128],
                       pst.h[:, 0:512].rearrange("p (a b) -> p a b", b=128), [pst], [vt])
        P.pop()
        wqk = [P.sb([128, 8, 256], BF16, "wqk") for _ in range(2)]
        qz = [[P.sb([128, T], BF16, "qz") for _ in range(2)] for _ in range(2)]
        kT2 = [P.sb([128, T], BF16, "kT") for _ in range(2)]
        for a_ in range(2):
            for b_ in range(2):
                P.pool(lambda e, t_=qz[a_][b_]: e.memset(t_.h[:], 0.0), [], [qz[a_][b_]])
        nrr = [0]
        nq = [0]
        nppa = [npp]

        def nextps():
            t_ = ps[nppa[0] % 4]
            nppa[0] += 1
            return t_

        def emit_proj(h):
            wt = wqk[h % 2]
            P.dma("pool", wt.h[:, :, 0:128], w_in[l][:, 3 * D + h * 128:3 * D + (h + 1) * 128].rearrange("(k p) f -> p k f", p=128), [], [wt])
            P.dma("pool", wt.h[:, :, 128:256], w_in[l][:, 4 * D + h * 128:4 * D + (h + 1) * 128].rearrange("(k p) f -> p k f", p=128), [], [wt])
            for which in (0, 1):
                for ci, (c0, w) in enumerate(CB):
                    if ci == 0 and which == 0 and last:
                        continue
                    pst = nextps()
                    proj_fm(wt, slice(which * 128, (which + 1) * 128), ci, pst)
                    if which == 0:
                        dsts = [(qz[h % 2][c], slice(c * 64, (c + 1) * 64)) for c in range(2)]
                    else:
                        dsts = [(kT2[h % 2], slice(0, 128))]
                    if ci == 0:
                        for dt_, sl in dsts:
                            P.copy("act", dt_.h[sl, 0:256], pst.h[sl, 0:256], [pst], [dt_])
                    else:
                        r0 = c0 - 256
                        tb_, tcc, tss = tb[nrr[0] % 2], tc_[nrr[0] % 2], ts_[nrr[0] % 2]
                        nrr[0] += 1
                        P.copy("act", tb_.h[:], pst.h[:, 0:512], [pst], [tb_])
                        P.tt("dve", tcc.h[:], pst.h[:, 0:512], rope.h[:, 0, r0:r0 + 512], ALU.mult, [pst, rope], [tcc])
                        pst2 = nextps()
                        P.mm(pst2.h[:, 0:512], rotm.h[:], tb_.h[:], True, True, [rotm, tb_], [pst2])
                        P.tt("dve", tss.h[:], pst2.h[:, 0:512], rope.h[:, 1, r0:r0 + 512], ALU.mult, [pst2, rope], [tss])
                        for dt_, sl in dsts:
                            P.tt("dve", dt_.h[sl, c0:c0 + 512], tcc.h[sl, :], tss.h[sl, :], ALU.add, [tcc, tss], [dt_])

        units = [(h, ci, c) for h in range(8) for ci in cbs for c in range(2)]

        def gen_A(i):
            h, ci, c = units[i]
            if i == 0:
                emit_proj(0)
            if ci == cbs[1] and c == 0 and h + 1 < 8:
                emit_proj(h + 1)
            c0, w = CB[ci]
            kts = [0, 1] if ci == 0 else list(range(18))
            e_ = eT[i % 2]
            qT, kT = qz[h % 2][c], kT2[h % 2]
            for kt in kts:
                pst = nextps()
                P.mm(pst.h[:, 0:w], kT.h[:, kt * 128:(kt + 1) * 128], qT.h[:, c0:c0 + w], True, True, [kT, qT], [pst])
                P.actf(e_.h[:, kt, 0:w], pst.h[:, 0:w], AF.Exp, [pst], [e_], scale=0.125)
                yield

        def bc(ap2, n):
            apl = [list(p_) for p_ in ap2.ap]
            return bass.AP(ap2.tensor, ap2.offset, apl + [[0, n]])

        def norm_ci(h, ci):
            c0, w = CB[ci]
            nqt = w // 128
            yh = ycT[h % 2]
            ob = obt[ci % 2]
            k_ = nq[0] % 2
            nq[0] += 1
            s_ = s16[k_]; o2_ = o2t[k_]; t1_ = t1t[k_]; yc_ = yc4[k_]
            rzv = s_.h[:, 0:2 * nqt].rearrange("p (a b) -> p a b", b=2)
            nl = s_.h[:, 8:8 + nqt]
            ssv = s_.h[:, 12:12 + nqt]
            P.dve(lambda e: e.reciprocal(out=rzv, in_=ob.h[:, 0:nqt, :, 128]), [ob], [s_], strict=True)
            P.ts("dve", nl, rzv[:, :, 1], nlam.h[:, l:l + 1], None, ALU.mult, None, [s_, nlam], [s_], strict=True)
            P.tt("dve", t1_.h[:, 0:nqt, :], ob.h[:, 0:nqt, 1, 0:128], bc(nl, 128), ALU.mult, [ob, s_], [t1_], strict=True)
            P.tt("dve", o2_.h[:, 0:nqt, :], ob.h[:, 0:nqt, 0, 0:128], bc(rzv[:, :, 0], 128), ALU.mult, [ob, s_], [o2_], strict=True)
            P.tt("dve", o2_.h[:, 0:nqt, :], o2_.h[:, 0:nqt, :], t1_.h[:, 0:nqt, :], ALU.add, [o2_, t1_], [o2_], strict=True)
            P.tt("dve", t1_.h[:, 0:nqt, :], o2_.h[:, 0:nqt, :], o2_.h[:, 0:nqt, :], ALU.mult, [o2_], [t1_], strict=True)
            P.dve(lambda e: e.reduce_sum(out=ssv, in_=t1_.h[:, 0:nqt, :], axis=mybir.AxisListType.X), [t1_], [s_], strict=True)
            P.actf(ssv, ssv, AF.Sqrt, [s_, epsk], [s_], bias=epsk.h[:], scale=1.0 / (128 * kk_ * kk_), strict=True)
            P.dve(lambda e: e.reciprocal(out=ssv, in_=ssv), [s_], [s_], strict=True)
            P.tt("dve", yc_.h[:, 0:nqt, :], o2_.h[:, 0:nqt, :], bc(ssv, 128), ALU.mult, [o2_, s_], [yc_], strict=True)
            pst = nextps()
            pv = pst.h[:, 0:256].bitcast(BF16)
            for qt in range(nqt):
                P.pe(lambda e, o=pv[:, qt * 128:(qt + 1) * 128], i=yc_.h[:, qt, :]: e.transpose(o, i, identb.h[:]), [yc_, identb], [pst])
            P.copy("dve", yh.h[:, c0:c0 + w], pv[:, 0:w], [pst], [yh])

        def gen_B(i):
            h, ci, c = units[i]
            c0, w = CB[ci]
            kts = [0, 1] if ci == 0 else list(range(18))
            nqt = w // 128
            e_ = eT[i % 2]
            for i_, kt in enumerate(kts):
                for qt in range(nqt):
                    pso = ps[4 + qt]
                    P.mm(pso.h[:, 0:130], e_.h[:, kt, qt * 128:(qt + 1) * 128], vt.h[:, kt, h, :], i_ == 0, i_ == len(kts) - 1, [e_, vt], [pso])
                yield
            for qt in range(nqt):
                P.copy("dve", obt[ci % 2].h[:, qt, c, :], ps[4 + qt].h[:, 0:130], [ps[4 + qt]], [obt[ci % 2]])
            if c == 1:
                norm_ci(h, ci)
                if ci == cbs[-1]:
                    yh = ycT[h % 2]
                    if not last:
                        P.dma("sp", ycS[:, h, 0:256], yh.h[:, 0:256], [yh], [])
                    P.dma("sp", ycS[:, h, 256:T], yh.h[:, 256:T], [yh], [])
            yield

        nun_ = len(units)
        if stop_after in ("attn_v",):
            nun_ = 0
        gb = iter(())
        for i in range(nun_ + 1):
            ga = gen_A(i) if i < nun_ else iter(())
            da = db = False
            while not (da and db):
                if not da:
                    try:
                        next(ga)
                    except StopIteration:
                        da = True
                if not db:
                    try:
                        next(gb)
                    except StopIteration:
                        db = True
            gb = gen_B(i) if i < nun_ else iter(())
        P.pop()
        if stop_after is not None and stop_after.startswith("attn"):
            break

        P.scope = f"L{l}_merge"
        P.push()
        yT3 = [P.sb([128, 8, T], BF16, "yT") for _ in range(3)]
        wb = [P.sb([128, 24, 128], BF16, "wb") for _ in range(2)]
        wg = [P.sb([128, 8, 3, 128], BF16, "wg") for _ in range(2)]
        ncb = len(cbs)
        gsb = [P.sb([128, 512], F32, "gsb") for _ in range(ncb)]
        mix = [P.sb([128, 512], F32, "mix") for _ in range(ncb)]
        mtmp = [P.sb([128, 512], F32, "mtmp") for _ in range(2)]
        mixb = [P.sb([128, 512], BF16, "mixb") for _ in range(ncb)]
        lo = 256 if last else 0
        for br, src in enumerate((yaS, ybS, ycS)):
            for k in range(8):
                P.dma("sp", yT3[br].h[:, k, lo:T], src[:, k, lo:T], [], [yT3[br]])
        for oc in range(8):
            wbt = wb[oc % 2]; wgt = wg[oc % 2]
            P.dma("pool", wbt.h[:], w_branch[l][:, oc * 128:(oc + 1) * 128].rearrange("(k p) f -> p k f", p=128), [], [wbt])
            for br in range(3):
                P.dma("pool", wgt.h[:, :, br, :], w_gate[l][:, br * D + oc * 128:br * D + (oc + 1) * 128].rearrange("(k p) f -> p k f", p=128), [], [wgt])
            for br in range(3):
                bg = take(ncb)
                group_mm(bg, cbs, 8, lambda k: wgt.h[:, k, br, :], lambda k, ci: hT.h[:, k, CB[ci][0]:CB[ci][0] + CB[ci][1]], wgt, hT)
                for j, ci in enumerate(cbs):
                    w = CB[ci][1]
                    P.actf(gsb[j].h[:, 0:w], bg[j].h[:, 0:w], AF.Sigmoid, [bg[j], bgate], [gsb[j]], bias=bgate.h[:, l, br * 8 + oc:br * 8 + oc + 1])
                bp = take(ncb)
                yb_ = yT3[br]
                group_mm(bp, cbs, 8, lambda k: wbt.h[:, br * 8 + k, :], lambda k, ci: yb_.h[:, k, CB[ci][0]:CB[ci][0] + CB[ci][1]], wbt, yb_)
                for j, ci in enumerate(cbs):
                    w = CB[ci][1]
                    if br == 0:
                        P.tt("dve", mix[j].h[:, 0:w], bp[j].h[:, 0:w], gsb[j].h[:, 0:w], ALU.mult, [bp[j], gsb[j]], [mix[j]])
                    else:
                        mt = mtmp[j % 2]
                        P.tt("dve", mt.h[:, 0:w], bp[j].h[:, 0:w], gsb[j].h[:, 0:w], ALU.mult, [bp[j], gsb[j]], [mt])
                        dst = mix[j] if br == 1 else mixb[j]
                        P.tt("dve", dst.h[:, 0:w], mix[j].h[:, 0:w], mt.h[:, 0:w], ALU.add, [mix[j], mt], [dst])
            for j, ci in enumerate(cbs):
                c0, w = CB[ci]
                P.dma("sp", mixS[:, oc, c0:c0 + w], mixb[j].h[:, 0:w], [mixb[j]], [])
        P.pop()
        if stop_after == "merge":
            break

        P.scope = f"L{l}_m2"
        P.push()
        wo = P.sb([128, 8, D], BF16, "wo")
        mxb = [P.sb([128, 8, 512], BF16, "mxb") for _ in range(2)]
        xb = [P.sb([128, 8, 512], F32, "xb2") for _ in range(2)]
        nt_ = norm_tiles()
        P.dma("pool", wo.h[:], w_out[l].rearrange("(k p) f -> p k f", p=128), [], [wo])
        npp = 0
        for n_, ci in enumerate(cbs):
            c0, w = CB[ci]
            j = 1 if ci == 0 else 0
            mb = mxb[n_ % 2]; b = xb[n_ % 2]
            P.dma("sp", mb.h[:, :, 0:w], mixS[:, :, c0:c0 + w], [], [mb])
            P.dma("sp", b.h[:, :, 0:w], xS[:, :, c0:c0 + w], [], [b])
            for oc in range(8):
                pst = ps[npp % 4]; npp += 1
                for k in range(8):
                    P.mm(pst.h[:, 0:w], wo.h[:, k, oc * 128:(oc + 1) * 128], mb.h[:, k, 0:w], k == 0, k == 7, [wo, mb], [pst])
                P.stt(b.h[:, oc, 0:w], pst.h[:, 0:w], mod(l, 2, oc, j), b.h[:, oc, 0:w], ALU.mult, ALU.add, [pst, mods, b], [b])
            P.dma("sp", xS[:, :, c0:c0 + w], b.h[:, :, 0:w], [b], [])
            norm_block(l, 2, ci, b, *nt_)
        P.pop()
        if stop_after == "m2":
            break

        P.scope = f"L{l}_ffn"
        P.push()
        uT = P.sb([128, NFF, T], BF16, "uT")
        w13 = [P.sb([128, 8, 2, 128], BF16, "w13") for _ in range(2)]
        w2t = [P.sb([128, NFF, 128], BF16, "w2t") for _ in range(2)]
        ncb = len(cbs)
        ssb = [P.sb([128, 512], F32, "ssb") for _ in range(2 * ncb)]
        xq = [P.sb([128, 512], F32, "xq") for _ in range(ncb + 1)]
        for oc in range(NFF):
            wt = w13[oc % 2]
            P.dma("pool", wt.h[:, :, 0, :], ffn_w1[l][:, oc * 128:(oc + 1) * 128].rearrange("(k p) f -> p k f", p=128), [], [wt])
            P.dma("pool", wt.h[:, :, 1, :], ffn_w3[l][:, oc * 128:(oc + 1) * 128].rearrange("(k p) f -> p k f", p=128), [], [wt])
            b1 = take(ncb)
            group_mm(b1, cbs, 8, lambda k: wt.h[:, k, 0, :], lambda k, ci: hT.h[:, k, CB[ci][0]:CB[ci][0] + CB[ci][1]], wt, hT)
            sset = ssb[(oc % 2) * ncb:(oc % 2 + 1) * ncb]
            for j, ci in enumerate(cbs):
                w = CB[ci][1]
                P.actf(sset[j].h[:, 0:w], b1[j].h[:, 0:w], AF.Silu, [b1[j]], [sset[j]])
            b3 = take(ncb)
            group_mm(b3, cbs, 8, lambda k: wt.h[:, k, 1, :], lambda k, ci: hT.h[:, k, CB[ci][0]:CB[ci][0] + CB[ci][1]], wt, hT)
            for j, ci in enumerate(cbs):
                c0, w = CB[ci]
                P.tt("dve", uT.h[:, oc, c0:c0 + w], b3[j].h[:, 0:w], sset[j].h[:, 0:w], ALU.mult, [b3[j], sset[j]], [uT])
        nx = 0
        for oc in range(8):
            wt = w2t[oc % 2]
            P.dma("pool", wt.h[:], ffn_w2[l][:, oc * 128:(oc + 1) * 128].rearrange("(k p) f -> p k f", p=128), [], [wt])
            xqs = []
            for j, ci in enumerate(cbs):
                c0, w = CB[ci]
                xq_ = xq[nx % (ncb + 1)]; nx += 1
                P.dma("sp", xq_.h[:, 0:w], xS[:, oc, c0:c0 + w], [], [xq_])
                xqs.append(xq_)
            bb = take(ncb)
            group_mm(bb, cbs, NFF, lambda k: wt.h[:, k, :], lambda k, ci: uT.h[:, k, CB[ci][0]:CB[ci][0] + CB[ci][1]], wt, uT)
            for j, ci in enumerate(cbs):
                c0, w = CB[ci]
                jx = 1 if ci == 0 else 0
                xq_ = xqs[j]
                P.stt(xq_.h[:, 0:w], bb[j].h[:, 0:w], mod(l, 5, oc, jx), xq_.h[:, 0:w], ALU.mult, ALU.add, [bb[j], mods, xq_], [xq_])
                P.dma("sp", xS[:, oc, c0:c0 + w], xq_.h[:, 0:w], [xq_], [])
        P.pop()

    if stop_after is None:
        P.scope = "final"
        P.push()
        xb = [P.sb([128, 8, 512], F32, "xbf") for _ in range(2)]
        sq = P.sb([128, 8, 512], BF16, "sqf")
        rt = P.sb([128, 512], F32, "rtf")
        rstd = P.sb([128, 512], F32, "rstdf")
        ykall = [P.sb([128, 8, 512], F32, "ykall") for _ in range(2)]
        otile = [P.sb([128, D], F32, "otile") for _ in range(4)]
        npp = 0
        for n_, ci in enumerate(range(1, 5)):
            c0, w = CB[ci]
            b = xb[n_ % 2]
            P.dma("sp", b.h[:], xS[:, :, c0:c0 + w], [], [b])
            pss = ps[4 + n_ % 2]
            P.actf(sq.h[:], b.h[:], AF.Square, [b], [sq])
            for k in range(8):
                P.mm(pss.h[:, 0:w], ones_bf.h[:], sq.h[:, k, :], k == 0, k == 7, [ones_bf, sq], [pss])
            P.actf(rt.h[:], pss.h[:, 0:w], AF.Sqrt, [pss, eps_t], [rt], bias=eps_t.h[:], scale=1.0 / D)
            P.dve(lambda e: e.reciprocal(out=rstd.h[:], in_=rt.h[:]), [rt], [rstd])
            yall = ykall[n_ % 2]
            for k in range(8):
                P.stt(yall.h[:, k, :], b.h[:, k, :], fing.h[:, k:k + 1], rstd.h[:], ALU.mult, ALU.mult, [b, fing, rstd], [yall])
            for t4 in range(4):
                for half in range(2):
                    pq = ps[npp % 4]; npp += 1
                    for kk in range(4):
                        k = half * 4 + kk
                        P.pe(lambda e, o=pq.h[:, kk * 128:(kk + 1) * 128], i=yall.h[:, k, t4 * 128:(t4 + 1) * 128]: e.transpose(o, i, identf.h[:]), [yall, identf], [pq])
                    P.copy("act" if npp % 2 else "dve", otile[t4].h[:, half * 512:(half + 1) * 512], pq.h[:, 0:512], [pq], [otile[t4]])
            for t4 in range(4):
                r0 = c0 - 256 + t4 * 128
                P.dma("sp", out[r0:r0 + 128, :], otile[t4].h[:], [otile[t4]], [])
        P.pop()
    else:
        P.barrier()

    P.emit()
    return nc


def _fm(v, nchunk):
    v = np.asarray(v, np.float32)
    lead = v.shape[:-1]
    r = v.reshape(lead + (nchunk, 128))
    r = np.moveaxis(r, -1, 0)
    return np.ascontiguousarray(r)


def _consts():
    bf = ml_dtypes.bfloat16
    n = np.arange(SEQ, dtype=np.float64)
    ang = 2.0 * np.pi * np.outer(n, n) / SEQ
    dftL = np.stack([np.cos(ang), np.sin(ang)]) / math.sqrt(SEQ)
    m = np.arange(CTX, dtype=np.float64)
    angc = 2.0 * np.pi * np.outer(m, m) / CTX
    dftC = np.concatenate([np.cos(angc), np.sin(angc)], axis=1) / math.sqrt(CTX)
    j = np.arange(128, dtype=np.float64)
    a128 = 2.0 * np.pi * np.outer(j, j) / 128
    d128 = np.concatenate([np.cos(a128), -np.sin(a128)], axis=1) / math.sqrt(128)
    d128p = np.concatenate([np.cos(a128), np.sin(a128)], axis=1) / math.sqrt(128)
    sgn = np.repeat(((-1.0) ** np.arange(128))[:, None], 2, axis=1) / math.sqrt(SEQ)
    rows = np.repeat(np.arange(SEQ // 64), 64).astype(np.float32)
    cols = np.tile(np.arange(64), SEQ // 64).astype(np.float32)
    inv = (np.float32(10000.0) ** (-np.arange(16, dtype=np.float32) / np.float32(16))).astype(np.float32)
    angr = np.concatenate([rows[:, None] * inv, cols[:, None] * inv], axis=-1).astype(np.float32)
    cos = np.cos(angr).astype(np.float32)
    sin = np.sin(angr).astype(np.float32)
    p = np.arange(128)
    dm = (p % 64) % 32
    rope = np.stack([cos[:, dm].T, sin[:, dm].T], axis=1)
    rotm = np.zeros((128, 128), np.float32)
    for q in range(128):
        d_ = q % 64
        if d_ < 32:
            rotm[q + 32, q] = -1.0
        else:
            rotm[q - 32, q] = 1.0
    return {
        "dftL": dftL.astype(bf), "dftC": dftC.astype(bf), "d128": d128.astype(bf), "d128p": d128p.astype(bf), "sgn": sgn.astype(bf),
        "ropeCS": np.ascontiguousarray(rope, dtype=np.float32).astype(bf), "rotm": rotm.astype(bf),
        "identf": np.eye(128, dtype=np.float32), "identb": np.eye(128, dtype=np.float32).astype(bf),
    }


def make_in_maps(inputs, cores):
    f = lambda a: np.ascontiguousarray(np.asarray(a, np.float32))
    shared = {
        "ada_w": f(inputs["ada_w"]), "w_in": f(inputs["w_in"]),
        "rnn_wr": f(inputs["rnn_wr"]), "rnn_wi": f(inputs["rnn_wi"]),
        "w_branch": f(inputs["w_branch"]), "w_gate": f(inputs["w_gate"]), "w_out": f(inputs["w_out"]),
        "ffn_w1": f(inputs["ffn_w1"]), "ffn_w3": f(inputs["ffn_w3"]), "ffn_w2": f(inputs["ffn_w2"]),
        "attn_lambda": f(inputs["attn_lambda"]).reshape(DEPTH, 256),
        "ada_b_fm": _fm(inputs["ada_b"], 48),
        "conv_w_fm": np.ascontiguousarray(np.transpose(_fm(inputs["rnn_conv_w"], 8), (0, 1, 3, 2))),
        "conv_b_fm": _fm(inputs["rnn_conv_b"], 8),
        "br_fm": _fm(inputs["rnn_br"], 8).reshape(128, 32), "bi_fm": _fm(inputs["rnn_bi"], 8).reshape(128, 32), "lam_fm": _fm(inputs["rnn_lambda"], 8).reshape(128, 32),
        "b_gate_fm": _fm(inputs["b_gate"], 24),
        "final_g_fm": _fm(inputs["final_g"], 8),
    }
    shared.update(_consts())
    x = f(inputs["x"]); ctx = f(inputs["ctx"]); c = f(inputs["c"]); c_ctx = f(inputs["c_ctx"])
    maps = []
    for b in cores:
        m = dict(shared)
        m["x"] = x[b]
        m["ctx"] = ctx[b]
        m["cond"] = np.ascontiguousarray(np.stack([_fm(c[b], 8), _fm(c_ctx, 8)], axis=-1))
        maps.append(m)
    return maps


def kernel(**inputs):
    nc = build_nc()
    in_maps = make_in_maps(inputs, list(range(8)))
    res = run_bass_kernel_spmd(nc, in_maps, core_ids=list(range(8)))
    return np.stack([np.asarray(r["out"], np.float32) for r in res.results], axis=0)
```
